# Optimizing a Trainium2 kernel written in Bass

```python
import math
import jax, jax.numpy as jnp
from jax import lax
import numpy as np

D_MODEL = 1024
BATCH = 16
SEQ = 2048
DEPTH = 2

CHUNK = 64
Q_BLOCK = 128
HEAD_DIM = 64
N_HEADS_FOX = D_MODEL // (4 * HEAD_DIM)
N_HEADS_DIFF = D_MODEL // (4 * 2 * HEAD_DIM)
DIFF_QK_DIM = HEAD_DIM
DIFF_V_DIM = 2 * HEAD_DIM
N_HEADS_CHK = D_MODEL // (4 * HEAD_DIM)
CHUNK_LOOKBACK = 8
MAX_REL_DIST = 128
D_FF = 2816
CONV_WIDTH = 3
RMS_EPS = 1e-6
NEG_INF = -1e30

FOX_W = N_HEADS_FOX * HEAD_DIM
DIFF_QK_W = N_HEADS_DIFF * 2 * DIFF_QK_DIM
DIFF_W = N_HEADS_DIFF * DIFF_V_DIM
CHK_W = N_HEADS_CHK * HEAD_DIM
MIX_WIDTH = FOX_W + DIFF_W + CHK_W
IN_SPLITS = (FOX_W, FOX_W, FOX_W, N_HEADS_FOX, DIFF_QK_W, DIFF_QK_W, DIFF_W, CHK_W, CHK_W, CHK_W)
N_IN = 3 * FOX_W + N_HEADS_FOX + 2 * DIFF_QK_W + DIFF_W + 3 * CHK_W

kernel_name = 'hybrid_fox_diff_chunk_convffn_encoder'


def rms_norm(x, g):
    xf = x.astype(jnp.float32)
    y = xf * lax.rsqrt(jnp.mean(xf * xf, axis=-1, keepdims=True) + RMS_EPS)
    return (y * g.astype(jnp.float32)).astype(x.dtype)


def split_columns(proj):
    outs = []
    start = 0
    for w in IN_SPLITS:
        outs.append(proj[..., start:start + w])
        start += w
    return outs


def alibi_slopes(n_heads):
    return jnp.asarray([2.0 ** (-8.0 * (i + 1) / n_heads) for i in range(n_heads)], jnp.float32)


def forgetting_attention(q, k, v, log_f):
    S, D = q.shape[1], q.shape[3]
    cum = jnp.transpose(jnp.cumsum(log_f, axis=1), (0, 2, 1))
    scale = D ** -0.5
    outs = []
    for blk in range(S // Q_BLOCK):
        q0, q1 = blk * Q_BLOCK, (blk + 1) * Q_BLOCK
        s = jnp.einsum('bqhd,bkhd->bhqk', q[:, q0:q1], k[:, :q1]).astype(jnp.float32) * scale
        s = s + cum[:, :, q0:q1, None] - cum[:, :, None, :q1]
        qpos = jnp.arange(q0, q1)[:, None]
        kpos = jnp.arange(q1)[None, :]
        s = jnp.where(kpos <= qpos, s, NEG_INF)
        p = jax.nn.softmax(s, axis=-1).astype(v.dtype)
        outs.append(jnp.einsum('bhqk,bkhd->bqhd', p, v[:, :q1]))
    return jnp.concatenate(outs, axis=1)


def differential_attention(q, k, v, lam, slopes):
    S, Dqk = q.shape[1], q.shape[4]
    scale = Dqk ** -0.5
    outs = []
    for blk in range(S // Q_BLOCK):
        q0, q1 = blk * Q_BLOCK, (blk + 1) * Q_BLOCK
        s = jnp.einsum('bqhcd,bkhcd->bhcqk', q[:, q0:q1], k[:, :q1]).astype(jnp.float32) * scale
        qpos = jnp.arange(q0, q1)[:, None]
        kpos = jnp.arange(q1)[None, :]
        dist = jnp.abs(qpos - kpos).astype(jnp.float32)
        s = s - slopes[None, :, None, None, None] * dist
        allowed = (kpos // CHUNK) <= (qpos // CHUNK)
        s = jnp.where(allowed, s, NEG_INF)
        p = jax.nn.softmax(s, axis=-1)
        a = (p[:, :, 0] - lam * p[:, :, 1]).astype(v.dtype)
        outs.append(jnp.einsum('bhqk,bkhd->bqhd', a, v[:, :q1]))
    return jnp.concatenate(outs, axis=1)


def chunked_band_attention(q, k, v, rel_table):
    B, S, H, D = q.shape
    nc = S // CHUNK
    band = (CHUNK_LOOKBACK + 1) * CHUNK
    qc = q.reshape(B, nc, CHUNK, H, D)

    def gather_band(t):
        tc = t.reshape(B, nc, CHUNK, H, D)
        tp = jnp.pad(tc, ((0, 0), (CHUNK_LOOKBACK, 0), (0, 0), (0, 0), (0, 0)))
        return jnp.concatenate([tp[:, j:j + nc] for j in range(CHUNK_LOOKBACK + 1)], axis=2)

    kb, vb = gather_band(k), gather_band(v)
    s = jnp.einsum('bcqhd,bckhd->bchqk', qc, kb).astype(jnp.float32) * (D ** -0.5)
    i = jnp.arange(CHUNK)[:, None]
    j = jnp.arange(band)[None, :]
    rel = jnp.clip(i + CHUNK_LOOKBACK * CHUNK - j, -(CHUNK - 1), MAX_REL_DIST)
    bias = rel_table.astype(jnp.float32)[:, rel + CHUNK - 1]
    s = s + bias[None, None]
    key_chunk = jnp.arange(nc)[:, None] - CHUNK_LOOKBACK + (jnp.arange(band) // CHUNK)[None, :]
    valid = key_chunk >= 0
    s = jnp.where(valid[None, :, None, None, :], s, NEG_INF)
    p = jax.nn.softmax(s, axis=-1).astype(v.dtype)
    out = jnp.einsum('bchqk,bckhd->bcqhd', p, vb)
    return out.reshape(B, S, H, D)


def conv_gated_mlp(h, w_up, conv_w, conv_b, w_down):
    uv = h @ w_up
    S = uv.shape[1]
    padded = jnp.pad(uv, ((0, 0), (CONV_WIDTH - 1, 0), (0, 0)))
    y = conv_b
    for tap in range(CONV_WIDTH):
        y = y + conv_w[tap] * padded[:, tap:tap + S]
    u, g = jnp.split(y, 2, axis=-1)
    return (jax.nn.silu(g) * u) @ w_down


def setup_inputs(seed: int = 0) -> dict:
    key = jax.random.key(seed)
    ks = jax.random.split(key, 16)

    def nrm(k, shape, scale):
        return jax.random.normal(k, shape, jnp.float32) * scale

    return {
        'x': nrm(ks[0], (BATCH, SEQ, D_MODEL), 1.0),
        'g_mix': 1.0 + nrm(ks[1], (DEPTH, D_MODEL), 0.02),
        'w_in': nrm(ks[2], (DEPTH, D_MODEL, N_IN), D_MODEL ** -0.5),
        'b_fox_f': 2.0 + nrm(ks[3], (DEPTH, N_HEADS_FOX), 0.5),
        'diff_lambda': nrm(ks[4], (DEPTH, 4, DIFF_QK_DIM), 0.1),
        'g_diff': 1.0 + nrm(ks[5], (DEPTH, DIFF_V_DIM), 0.02),
        'rel_bias': nrm(ks[6], (DEPTH, N_HEADS_CHK, CHUNK + MAX_REL_DIST), 0.1),
        'w_out': nrm(ks[7], (DEPTH, MIX_WIDTH, D_MODEL), MIX_WIDTH ** -0.5),
        'g_ffn': 1.0 + nrm(ks[8], (DEPTH, D_MODEL), 0.02),
        'w_ffn_in': nrm(ks[9], (DEPTH, D_MODEL, 2 * D_FF), D_MODEL ** -0.5),
        'conv_w': nrm(ks[10], (DEPTH, CONV_WIDTH, 2 * D_FF), CONV_WIDTH ** -0.5),
        'conv_b': nrm(ks[11], (DEPTH, 2 * D_FF), 0.01),
        'w_ffn_out': nrm(ks[12], (DEPTH, D_FF, D_MODEL), D_FF ** -0.5),
        'g_final': 1.0 + nrm(ks[13], (D_MODEL,), 0.02),
    }


def reference(x, g_mix, w_in, b_fox_f, diff_lambda, g_diff, rel_bias, w_out,
              g_ffn, w_ffn_in, conv_w, conv_b, w_ffn_out, g_final):
    B, S, _ = x.shape
    slopes = alibi_slopes(N_HEADS_DIFF)
    for l in range(DEPTH):
        h = rms_norm(x, g_mix[l])
        proj = h @ w_in[l]
        fq, fk, fv, ff, dq, dk, dv, cq, ck, cv = split_columns(proj)

        log_f = jax.nn.log_sigmoid(ff.astype(jnp.float32) + b_fox_f[l].astype(jnp.float32))
        out_a = forgetting_attention(
            fq.reshape(B, S, N_HEADS_FOX, HEAD_DIM),
            fk.reshape(B, S, N_HEADS_FOX, HEAD_DIM),
            fv.reshape(B, S, N_HEADS_FOX, HEAD_DIM), log_f)

        lambda_init = 0.8 - 0.6 * math.exp(-0.3 * l)
        lam_vecs = diff_lambda[l].astype(jnp.float32)
        lam = (jnp.exp(jnp.sum(lam_vecs[0] * lam_vecs[1]))
               - jnp.exp(jnp.sum(lam_vecs[2] * lam_vecs[3])) + lambda_init)
        out_b = differential_attention(
            dq.reshape(B, S, N_HEADS_DIFF, 2, DIFF_QK_DIM),
            dk.reshape(B, S, N_HEADS_DIFF, 2, DIFF_QK_DIM),
            dv.reshape(B, S, N_HEADS_DIFF, DIFF_V_DIM), lam, slopes)
        out_b = rms_norm(out_b, g_diff[l]) * (1.0 - lambda_init)

        out_c = chunked_band_attention(
            cq.reshape(B, S, N_HEADS_CHK, HEAD_DIM),
            ck.reshape(B, S, N_HEADS_CHK, HEAD_DIM),
            cv.reshape(B, S, N_HEADS_CHK, HEAD_DIM), rel_bias[l])

        mix = jnp.concatenate([out_a.reshape(B, S, FOX_W),
                               out_b.reshape(B, S, DIFF_W),
                               out_c.reshape(B, S, CHK_W)], axis=-1)
        x = x + mix @ w_out[l]

        x = x + conv_gated_mlp(rms_norm(x, g_ffn[l]), w_ffn_in[l], conv_w[l], conv_b[l], w_ffn_out[l])
    return rms_norm(x, g_final)
```

```python
import math
import numpy as np
import concourse.bass as bass
import concourse.mybir as mybir
from concourse.bass_utils import run_bass_kernel_spmd

F32 = mybir.dt.float32
BF16 = mybir.dt.bfloat16
AF = mybir.ActivationFunctionType
ALU = mybir.AluOpType

POOL_FFN = False
S = 2048
NT = 4
D = 1024
KC = 8
NFC = 22
NEG = -30000.0
EPS = 1e-6
SLOPES = [2.0 ** (-8.0 * (i + 1) / 2) for i in range(2)]


class Op:
    __slots__ = ("eng", "fn", "deps", "sig", "count", "lane", "lane_val", "is_dma")


class Sched:
    ENGS = ("pe", "act", "dve", "pool", "sp")

    def __init__(self, nc):
        self.nc = nc
        self.ops = []
        self.last_write = {}
        self.readers = {}
        self.lanes = {}
        self.lane_cnt = {}
        self.pending_dma = []
        self.bar_deps = []
        self.need_bar = {e: False for e in self.ENGS}

    def add(self, eng, fn, reads=(), writes=(), lane=None):
        i = len(self.ops)
        op = Op()
        op.eng = eng
        op.fn = fn
        op.sig = False
        op.count = 0
        op.is_dma = lane is not None
        op.lane = lane
        op.lane_val = 0
        deps = {}
        for r in reads:
            lw = self.last_write.get(r)
            if lw is not None:
                deps[lw] = "raw"
        for w in writes:
            lw = self.last_write.get(w)
            if lw is not None and lw not in deps:
                deps[lw] = "waw"
            rd = self.readers.get(w)
            if rd:
                for k, j in rd.items():
                    if isinstance(j, list):
                        for jj in j:
                            if jj not in deps:
                                deps[jj] = "war"
                    elif j not in deps:
                        deps[j] = "war"
        if self.need_bar[eng]:
            for j in self.bar_deps:
                deps[j] = "raw"
            self.need_bar[eng] = False
        for r in reads:
            d = self.readers.setdefault(r, {})
            if op.is_dma:
                d.setdefault("dma", []).append(i)
            else:
                d[eng] = i
        for w in writes:
            self.last_write[w] = i
            self.readers[w] = {}
        if op.is_dma:
            self.lane_cnt[lane] = self.lane_cnt.get(lane, 0) + 1
            op.lane_val = 16 * self.lane_cnt[lane]
            self.pending_dma.append(i)
        op.deps = deps
        self.ops.append(op)
        return i

    def barrier(self, markers):
        self.bar_deps = list(markers) + list(self.pending_dma)
        self.pending_dma = []
        for e in self.ENGS:
            self.need_bar[e] = True

    def _needs_sync(self, a, b, kind):
        if a.is_dma or b.is_dma:
            return True
        if a.eng != b.eng:
            return True
        if a.eng != "pe":
            return True
        return False

    def finalize(self):
        ops = self.ops
        for b in ops:
            for j, kind in b.deps.items():
                a = ops[j]
                if self._needs_sync(a, b, kind) and not a.is_dma:
                    a.sig = True
        cnt = {e: 0 for e in self.ENGS}
        for o in ops:
            if o.sig:
                cnt[o.eng] += 1
                o.count = cnt[o.eng]

    def emit(self, final_lanes):
        nc = self.nc
        self.finalize()
        ops = self.ops
        sems = {e: nc.alloc_semaphore("sem_" + e) for e in self.ENGS}
        lane_sems = {l: nc.alloc_semaphore("lane_" + l) for l in self.lane_cnt}
        per_eng = {e: [] for e in self.ENGS}
        for i, o in enumerate(ops):
            per_eng[o.eng].append(i)

        def replay(ename, e):
            waited = {}
            for i in per_eng[ename]:
                o = ops[i]
                need = {}
                for j, kind in o.deps.items():
                    a = ops[j]
                    if not self._needs_sync(a, o, kind):
                        continue
                    if a.is_dma:
                        key, val = ("L", a.lane), a.lane_val
                    else:
                        key, val = ("E", a.eng), a.count
                    if val > need.get(key, 0):
                        need[key] = val
                for key, val in need.items():
                    if val > waited.get(key, 0):
                        waited[key] = val
                        sem = lane_sems[key[1]] if key[0] == "L" else sems[key[1]]
                        e.wait_ge(sem, val)
                inst = o.fn(e)
                if o.is_dma:
                    inst.then_inc(lane_sems[o.lane], 16)
                elif o.sig:
                    inst.then_inc(sems[ename], 1)
            if ename == "sp":
                for l in final_lanes:
                    e.wait_ge(lane_sems[l], 16 * self.lane_cnt[l])

        with nc.Block() as block:
            @block.tensor
            def _(e):
                replay("pe", e)

            @block.scalar
            def _(e):
                replay("act", e)

            @block.vector
            def _(e):
                replay("dve", e)

            @block.gpsimd
            def _(e):
                replay("pool", e)

            @block.sync
            def _(e):
                replay("sp", e)


def _win_perm():
    cols = []
    for r in range(2):
        a, b = 2 * r, 2 * r + 1
        cols += list(range(a * 64, a * 64 + 64)) + [768 + a]
        cols += list(range(b * 64, b * 64 + 64)) + [768 + b]
        cols += list(range(256 + a * 64, 256 + a * 64 + 64))
        cols += list(range(256 + b * 64, 256 + b * 64 + 64))
        cols += list(range(512 + a * 64, 512 + a * 64 + 128))
    base = 772
    cols += list(range(base, base + 768))
    cols += list(range(base + 768, base + 1536))
    assert len(cols) == 2308 and len(set(cols)) == 2308
    return np.asarray(cols)


ROUND_COL0 = [0, 386, 772, 1540]
ROUND_NCOL = [386, 386, 768, 768]


def _host_consts():
    k = np.arange(128)[:, None]
    q = np.arange(128)[None, :]
    c = {}
    c["ident"] = np.eye(128, dtype=np.float32)
    c["cmf"] = np.where(q >= k, 0.0, 8 * NEG).astype(np.float32)
    td = np.zeros((128, 2, 128), np.float32)
    for h in range(2):
        t = np.where(q >= k, 0.0, -2.0 * SLOPES[h] * (k - q) * 8.0)
        t = np.where((k // 64) > (q // 64), 8 * NEG, t)
        td[:, h, :] = t
    c["td"] = td
    bd = np.zeros((128, 32), np.float32)
    for h in range(2):
        for jj in range(16):
            bd[:, h * 16 + jj] = SLOPES[h] * (np.arange(128) + 128.0 * (jj - 12))
    c["bd"] = bd
    idx = np.zeros((5, 128, 128), np.int64)
    msk = np.zeros((128, 5, 4, 128), np.float32)
    for j in range(5):
        tms = 128 * (4 - j) + q - k
        idx[j] = np.clip(tms, -63, 128) + 63
        qc = q // 64
        kc = k // 64 + 2 * (j - 4)
        valid = (kc <= qc) & (kc >= qc - 8)
        msk[:, j, :, :] = np.where(valid, 0.0, 8 * NEG)[:, None, :]
    c["relidx"] = idx
    c["relmask"] = msk
    return c


def build_program(n_seq=2, layers=(0, 1), final_norm=True):
    nc = bass.Bass("TRN2", target_bir_lowering=False)
    dt_in = lambda name, shape: nc.dram_tensor(name, list(shape), F32, kind="ExternalInput").ap()
    x_in = dt_in("xT_in", (n_seq, 128, KC, S))
    win = dt_in("win", (2, 128, KC, 2308))
    wout = dt_in("wout", (2, 128, 6, D))
    wup = dt_in("wup", (2, 128, NFC, KC, 256))
    wdn = dt_in("wdn", (2, 128, 2, 8, 11, 128))
    gcols_in = dt_in("gcols", (128, 40))
    convp_in = dt_in("convp", (128, 2, 44, 4))
    gdiff_in = dt_in("gdiff", (128, 2))
    bfox_in = dt_in("bfox", (1, 8))
    dlam_in = dt_in("dlam", (128, 2, 256))
    relg_in = dt_in("relg", (2, 128, 5, 4, 128))
    relmask_in = dt_in("relmask", (128, 5, 4, 128))
    ident_in = dt_in("ident", (128, 128))
    cmf_in = dt_in("cmf", (128, 128))
    td_in = dt_in("td", (128, 2, 128))
    bd_in = dt_in("bd", (128, 32))
    out_t = nc.dram_tensor("outT", [n_seq, 128, KC, S], F32, kind="ExternalOutput").ap()

    sc = Sched(nc)
    off = [(nc.sbuf_base + 31) // 32 * 32]

    def sb(name, shape, dt, at=None):
        nbytes = int(np.prod(shape[1:])) * (4 if dt == F32 else 2)
        if at is None:
            o = off[0]
            off[0] += (nbytes + 31) // 32 * 32
        else:
            o = at
        return nc.alloc_sbuf_tensor_at(name, list(shape), dt, offset=o)

    xT = sb("xT", [128, KC, S], F32)
    hT = sb("hT", [128, KC, S], BF16)
    r0 = off[0]
    QT = sb("QT", [128, 2, S], BF16)
    KT = sb("KT", [128, 2, S], BF16)
    VT = sb("VT", [128, 16, 386], BF16)
    mixt_off = off[0]
    MIXT = sb("MIXT", [128, 2, S], BF16)
    WPART = sb("WPART", [128, 2, KC, 768], BF16)
    WOPART = sb("WOPART", [128, 2, 2, D], BF16)
    pt_off = off[0]
    PT = sb("PT", [128, 8, 512], BF16)
    r_end_attn = off[0]
    GATE = sb("GATE", [128, S], F32, at=pt_off)
    off[0] = r0
    ACTT = sb("ACTT", [128, 12, 1024], BF16)
    WUP = sb("WUP", [128, 3, KC, 256], BF16)
    WDN = sb("WDN", [128, 4, 11, 128], BF16)
    UVR = sb("UVR", [128, 2, 2, 1032], F32)
    YB = sb("YB", [128, 2, 2, 512], F32)
    SIL = sb("SIL", [128, 2, 512], F32)
    assert off[0] <= r_end_attn, (off[0], r_end_attn)
    off[0] = r_end_attn
    tmp_off = off[0]
    TMP = sb("TMP", [128, 8, 512], F32)
    GATE2 = sb("GATE2", [128, S], F32, at=tmp_off)
    TMPB = TMP
    SQB = sb("SQB", [128, 2, 512], BF16)
    IDENT = sb("IDENT", [128, 128], BF16)
    ONESB = sb("ONESB", [128, 128], BF16)
    ONESF = sb("ONESF", [128, 128], F32)
    NEGONE = sb("NEGONE", [128, 2], F32)
    CMF = sb("CMF", [128, 128], BF16)
    TDT = sb("TDT", [128, 2, 128], BF16)
    BDT = sb("BDT", [128, 32], F32)
    TC = sb("TC", [128, 5, 512], BF16)
    TCS = sb("TCS", [128, 2, 512], F32)
    GCOLS = sb("GCOLS", [128, 40], F32)
    CONVP = sb("CONVP", [128, 2, 44, 4], F32)
    GDIFF = sb("GDIFF", [128, 2], F32)
    GDS = sb("GDS", [128, 2], F32)
    BFOX = sb("BFOX", [128, 8], F32)
    DLAM = sb("DLAM", [128, 2, 256], F32)
    LAMT = sb("LAMT", [128, 2, 128], F32)
    LAMS = sb("LAMS", [128, 8], F32)
    NEGCUM = sb("NEGCUM", [128, 2, 16], F32)
    HALO = sb("HALO", [128, 44, 2], F32)
    SCR = sb("SCR", [128, 8], F32)
    assert off[0] <= nc.sbuf_base + nc.sbuf_bytes_remaining, (off[0], nc.sbuf_base, nc.sbuf_bytes_remaining)

    PS = [nc.alloc_psum_tensor("ps%d" % b, [128, 512], F32) for b in range(8)]
    ring = {"s": [0, 1, 2, 3], "a": [4, 5, 6, 7], "q": [0, 1, 2], "m": [3], "u": [0, 1, 2, 3, 4, 5], "d": [6, 7]}
    ring_pos = {k_: 0 for k_ in ring}

    def bank(r):
        b = ring[r][ring_pos[r] % len(ring[r])]
        ring_pos[r] += 1
        return b

    A = sc.add

    def dma(eng, out, in_, lane, reads=(), writes=()):
        return A(eng, lambda e, out=out, in_=in_: e.dma_start(out=out, in_=in_), reads=reads, writes=writes, lane=lane)

    dma("pool", IDENT[:, :], ident_in[:, :], "c0", writes=[("c", "ident")])
    dma("pool", CMF[:, :], cmf_in[:, :], "c1", writes=[("c", "cmf")])
    dma("pool", TDT[:, :, :], td_in[:, :, :], "c2", writes=[("c", "td")])
    dma("sp", BDT[:, :], bd_in[:, :], "c3", writes=[("c", "bd")])
    dma("sp", GCOLS[:, :], gcols_in[:, :], "c4", writes=[("c", "gcols")])
    dma("sp", CONVP[:, :, :, :], convp_in[:, :, :, :], "c5", writes=[("c", "convp")])
    dma("sp", GDIFF[:, :], gdiff_in[:, :], "c6", writes=[("c", "gdiff")])
    dma("sp", BFOX[64:65, :], bfox_in[:, :], "c7", writes=[("c", "bfox")])
    dma("sp", DLAM[:, :, :], dlam_in[:, :, :], "c8", writes=[("c", "dlam")])
    A("pool", lambda e: e.memset(ONESB[:, :], 1.0), writes=[("c", "onesb")])
    A("pool", lambda e: e.memset(ONESF[:, :], 1.0), writes=[("c", "onesf")])
    A("pool", lambda e: e.memset(NEGONE[:, :], -1.0), writes=[("c", "negone")])
    A("pool", lambda e: e.memset(SCR[:, :], 0.0), writes=[("scr", "act"), ("scr", "dve"), ("scr", "zero")])

    def markers():
        m1 = A("act", lambda e: e.activation(out=SCR[0:1, 0:1], in_=SCR[0:1, 2:3], func=AF.Copy),
               reads=[("scr", "zero")], writes=[("scr", "act")])
        m2 = A("dve", lambda e: e.tensor_copy(out=SCR[0:1, 4:5], in_=SCR[0:1, 6:7]),
               reads=[("scr", "zero")], writes=[("scr", "dve")])
        return [m1, m2]

    def mm(out, lhsT, rhs, start, stop, reads, writes):
        return A("pe", lambda e, out=out, lhsT=lhsT, rhs=rhs, start=start, stop=stop:
                 e.matmul(out, lhsT, rhs, start=start, stop=stop), reads=reads, writes=writes)

    def rmsnorm(gcol0, dst_kind, tts=None):
        for tt in (range(NT) if tts is None else tts):
            ts = slice(tt * 512, (tt + 1) * 512)
            b = bank("s")
            for c in range(KC):
                sl = c % 2
                A("act", lambda e, c=c, sl=sl, ts=ts: e.activation(out=SQB[:, sl, :], in_=xT[:, c, ts], func=AF.Square),
                  reads=[("xT", c, tt)], writes=[("sqb", sl)])
                mm(PS[b][:, :], ONESB[:, :], SQB[:, sl, :], c == 0, c == KC - 1,
                   reads=[("sqb", sl), ("c", "onesb"), ("ps", b)], writes=[("ps", b)])
            A("act", lambda e, b=b: e.activation(out=TMP[:, 0, :], in_=PS[b][:, :], func=AF.Sqrt, bias=EPS, scale=1.0 / D),
              reads=[("ps", b)], writes=[("tmp", 0)])
            A("dve", lambda e: e.reciprocal(out=TMP[:, 1, :], in_=TMP[:, 0, :]),
              reads=[("tmp", 0)], writes=[("tmp", 1)])
            for c in range(KC):
                if dst_kind == "h":
                    A("dve", lambda e, c=c, ts=ts: e.scalar_tensor_tensor(
                        out=hT[:, c, ts], in0=xT[:, c, ts], scalar=GCOLS[:, gcol0 + c:gcol0 + c + 1], in1=TMP[:, 1, :],
                        op0=ALU.mult, op1=ALU.mult),
                      reads=[("xT", c, tt), ("tmp", 1), ("c", "gcols")], writes=[("hT", c, tt)])
                else:
                    A("dve", lambda e, c=c, ts=ts: e.scalar_tensor_tensor(
                        out=xT[:, c, ts], in0=xT[:, c, ts], scalar=GCOLS[:, gcol0 + c:gcol0 + c + 1], in1=TMP[:, 1, :],
                        op0=ALU.mult, op1=ALU.mult),
                      reads=[("xT", c, tt), ("tmp", 1), ("c", "gcols")], writes=[("xT", c, tt)])

    wslot = [0]

    def load_round_weights(l, r):
        sl = wslot[0] % 2
        wslot[0] += 1
        n = ROUND_NCOL[r]
        c0 = ROUND_COL0[r]
        dma("pool", WPART[:, sl, :, 0:n], win[l, :, :, c0:c0 + n], "wp%d" % sl, writes=[("wpart", sl)])
        mc0 = [0, 0, 2, 4][r]
        nmc = [0, 2, 2, 2][r]
        if nmc:
            dma("pool", WOPART[:, sl, 0:nmc, :], wout[l, :, mc0:mc0 + nmc, :], "wo%d" % sl, writes=[("wopart", sl)])
        return sl

    def proj_fm(sl, col0, m, dst_fn, extra_fn=None, wreads=()):
        for tt in range(NT):
            ts = slice(tt * 512, (tt + 1) * 512)
            b = bank("s")
            for c in range(KC):
                mm(PS[b][0:m, :], WPART[:, sl, c, col0:col0 + m], hT[:, c, ts], c == 0, c == KC - 1,
                   reads=[("wpart", sl), ("hT", c, tt), ("ps", b)], writes=[("ps", b)])
            m64 = min(m, 64) if extra_fn is not None else m
            out_ap, res, p0, p1 = dst_fn(tt)
            A("dve", lambda e, b=b, out_ap=out_ap, p0=p0, p1=p1: e.tensor_copy(out=out_ap, in_=PS[b][p0:p1, :]),
              reads=[("ps", b)], writes=[res])
            if extra_fn is not None:
                extra_fn(tt, b)

    def proj_tm(sl, col0, ncol, evac):
        ntb = 512 // ncol
        for g in range(16 // ntb):
            b = bank("s")
            for i in range(ntb):
                tb = g * ntb + i
                for c in range(KC):
                    mm(PS[b][:, i * ncol:(i + 1) * ncol], hT[:, c, tb * 128:(tb + 1) * 128], WPART[:, sl, c, col0:col0 + ncol],
                       c == 0, c == KC - 1,
                       reads=[("wpart", sl), ("hT", c, tb // 4), ("ps", b)], writes=[("ps", b)])
            evac(g * ntb, ntb, b)

    def out_proj_tt(sl, nmc, tt):
        ts = slice(tt * 512, (tt + 1) * 512)
        for dc in range(KC):
            b = bank("s")
            for mc in range(nmc):
                mm(PS[b][:, :], WOPART[:, sl, mc, dc * 128:(dc + 1) * 128], MIXT[:, mc, ts], mc == 0, mc == nmc - 1,
                   reads=[("wopart", sl), ("mixt", mc, tt), ("ps", b)], writes=[("ps", b)])
            A("dve", lambda e, b=b, dc=dc, ts=ts: e.tensor_tensor(out=xT[:, dc, ts], in0=PS[b][:, :], in1=xT[:, dc, ts], op=ALU.add),
              reads=[("ps", b), ("xT", dc, tt)], writes=[("xT", dc, tt)])

    ptpos = [0]

    def next_pt():
        s_ = ptpos[0] % 8
        ptpos[0] += 1
        return s_

    LOOK = 2

    def run_steps(steps, DEFER=2):
        n = len(steps)
        pending = []
        for i in range(min(LOOK, n)):
            steps[i][0]()
        for i in range(n):
            steps[i][1]()
            if i + LOOK < n:
                steps[i + LOOK][0]()
            due_now = [p_ for p_ in pending if p_[0] <= i]
            pending = [p_ for p_ in pending if p_[0] > i]
            for (due, fn) in due_now:
                r_ = fn()
                if r_ is not None:
                    pending.append((i + DEFER, r_))
            if steps[i][2] is not None:
                steps[i][2]()
            if steps[i][3] is not None:
                cont = steps[i][3]()
                if cont is not None:
                    pending.append((i + DEFER, cont))
        while pending:
            (due, fn) = pending.pop(0)
            r_ = fn()
            if r_ is not None:
                pending.append((due, r_))

    def causal_steps(qt, k_ap, q_ap, krows, tab_ap, tab_res, bias_fn, pv_list, qres, kres_fn, vres_fn, xreads=(), fin=None):
        nkb = 4 * qt + 4
        steps = []
        for kb in range(nkb):
            i = kb - 4 * qt
            c0 = max(i, 0) * 128
            st = {}

            def A_(kb=kb, i=i, c0=c0, st=st):
                b = bank("q")
                st["b"] = b
                mm(PS[b][:, c0:512], k_ap(kb), q_ap(c0), True, i < 0,
                   reads=[kres_fn(kb), qres, ("ps", b)] + list(xreads), writes=[("ps", b)])
                if i >= 0:
                    mm(PS[b][:, c0:c0 + 128], IDENT[:, :], tab_ap, False, True,
                       reads=[("c", "ident"), tab_res, ("ps", b)], writes=[("ps", b)])

            def B_(kb=kb, c0=c0, st=st):
                b = st["b"]
                p = next_pt()
                st["p"] = p
                bias_ap, bias_res = bias_fn(kb)
                A("act", lambda e, b=b, p=p, c0=c0, bias_ap=bias_ap: e.activation(
                    out=PT[:, p, c0:512], in_=PS[b][:, c0:512], func=AF.Exp, bias=bias_ap, scale=0.125),
                  reads=[("ps", b), bias_res], writes=[("pt", p)])

            def C_(kb=kb, c0=c0, st=st):
                p = st["p"]
                for (ob, rows, lhs_fn) in pv_list:
                    mm(PS[ob][0:rows, c0:512], lhs_fn(kb), PT[:, p, c0:512], kb == 0, kb == nkb - 1,
                       reads=[("pt", p), vres_fn(kb), ("ps", ob)], writes=[("ps", ob)])

            steps.append([A_, B_, C_, None])
        steps[-1][3] = fin
        return steps


    def norm_even_odd(ob_e, ob_o, ncols, dst_e, dst_o, dres):
        A("dve", lambda e: e.reciprocal(out=TMP[64:65, 2, 0:ncols], in_=PS[ob_e][64:65, 0:ncols]),
          reads=[("ps", ob_e)], writes=[("tmp", 2)])
        A("dve", lambda e: e.reciprocal(out=TMP[0:1, 4, 0:ncols], in_=PS[ob_o][0:1, 0:ncols]),
          reads=[("ps", ob_o)], writes=[("tmp", 4)])

        def part2():
            b = bank("m")
            mm(PS[b][0:64, 0:ncols], ONESF[64:65, 0:64], TMP[64:65, 2, 0:ncols], True, True,
               reads=[("tmp", 2), ("c", "onesf"), ("ps", b)], writes=[("ps", b)])
            A("dve", lambda e, b=b: e.tensor_copy(out=TMP[0:64, 3, 0:ncols], in_=PS[b][0:64, 0:ncols]),
              reads=[("ps", b)], writes=[("tmp", 3)])
            dst_e(lambda out_ap, c_lo, c_hi: A(
                "dve", lambda e, out_ap=out_ap, c_lo=c_lo, c_hi=c_hi: e.tensor_tensor(
                    out=out_ap, in0=PS[ob_e][0:64, c_lo:c_hi], in1=TMP[0:64, 3, c_lo:c_hi], op=ALU.mult),
                reads=[("ps", ob_e), ("tmp", 3)], writes=dres))
            b2 = bank("m")
            mm(PS[b2][:, 0:ncols], ONESF[0:1, 0:128], TMP[0:1, 4, 0:ncols], True, True,
               reads=[("tmp", 4), ("c", "onesf"), ("ps", b2)], writes=[("ps", b2)])
            A("dve", lambda e, b2=b2: e.tensor_copy(out=TMP[64:128, 5, 0:ncols], in_=PS[b2][64:128, 0:ncols]),
              reads=[("ps", b2)], writes=[("tmp", 5)])
            dst_o(lambda out_ap, c_lo, c_hi: A(
                "dve", lambda e, out_ap=out_ap, c_lo=c_lo, c_hi=c_hi: e.tensor_tensor(
                    out=out_ap, in0=PS[ob_o][64:128, c_lo:c_hi], in1=TMP[64:128, 5, c_lo:c_hi], op=ALU.mult),
                reads=[("ps", ob_o), ("tmp", 5)], writes=dres))
        return part2

    def init_aug(npairs, fox):
        allkt = [("kt", i_, t_) for i_ in range(2) for t_ in range(NT)]
        allvt = [("vt", g_) for g_ in range(4)]
        if fox:
            A("dve", lambda e: e.memset(KT[64:65, :, :], 1.0), writes=allkt + [("kaug",)])
        for p in range(npairs):
            o = p * 193
            A("dve", lambda e, o=o: e.memset(VT[:, :, o + 64:o + 66], 1.0), writes=allvt)
            A("dve", lambda e, o=o: e.memset(VT[:, :, o + 66:o + 129], 0.0), writes=allvt)

    for s in range(n_seq):
        for tt in range(NT):
            dma("sp", xT[:, :, tt * 512:(tt + 1) * 512], x_in[s, :, :, tt * 512:(tt + 1) * 512], "x%d" % tt,
                writes=[("xT", c, tt) for c in range(KC)])
        for l in layers:
            lambda_init = 0.8 - 0.6 * math.exp(-0.3 * l)
            A("dve", lambda e, l=l: e.tensor_tensor(out=LAMT[:, 0, 0:64], in0=DLAM[:, l, 0:64], in1=DLAM[:, l, 64:128], op=ALU.mult),
              reads=[("c", "dlam")], writes=[("lamt", 0)])
            A("dve", lambda e, l=l: e.tensor_tensor(out=LAMT[:, 1, 0:64], in0=DLAM[:, l, 128:192], in1=DLAM[:, l, 192:256], op=ALU.mult),
              reads=[("c", "dlam")], writes=[("lamt", 1)])
            A("dve", lambda e: e.reduce_sum(out=LAMS[:, 0:1], in_=LAMT[:, 0, 0:64], axis=mybir.AxisListType.X),
              reads=[("lamt", 0)], writes=[("lams", 0)])
            A("dve", lambda e: e.reduce_sum(out=LAMS[:, 1:2], in_=LAMT[:, 1, 0:64], axis=mybir.AxisListType.X),
              reads=[("lamt", 1)], writes=[("lams", 1)])
            A("act", lambda e: e.activation(out=LAMS[:, 2:4], in_=LAMS[:, 0:2], func=AF.Exp),
              reads=[("lams", 0), ("lams", 1)], writes=[("lams", 2)])
            A("dve", lambda e: e.tensor_tensor(out=LAMS[:, 4:5], in0=LAMS[:, 3:4], in1=LAMS[:, 2:3], op=ALU.subtract),
              reads=[("lams", 2)], writes=[("lams", 4)])
            A("dve", lambda e, li=lambda_init: e.tensor_scalar(out=LAMS[:, 5:6], in0=LAMS[:, 4:5], scalar1=-li, scalar2=None, op0=ALU.add),
              reads=[("lams", 4)], writes=[("neglam",)])
            A("dve", lambda e, l=l, li=lambda_init: e.tensor_scalar(out=GDS[:, 0:1], in0=GDIFF[:, l:l + 1], scalar1=1.0 - li, scalar2=None, op0=ALU.mult),
              reads=[("c", "gdiff")], writes=[("gds",)])
            for j in range(5):
                dma("sp", TCS[:, 0, :], relg_in[l, :, j, :, :], "tc0", writes=[("tcs", 0)])
                dma("sp", TCS[:, 1, :], relmask_in[:, j, :, :], "tc1", writes=[("tcs", 1)])
                A("dve", lambda e, j=j: e.scalar_tensor_tensor(
                    out=TC[:, j, :], in0=TCS[:, 0, :], scalar=8.0, in1=TCS[:, 1, :], op0=ALU.mult, op1=ALU.add),
                  reads=[("tcs", 0), ("tcs", 1)], writes=[("tc", j)])

            rmsnorm(l * 16, "h")

            for r in range(2):
                if r == 0:
                    sl_next = load_round_weights(l, 0)
                sl = sl_next
                init_aug(1, True)
                allpt = [("pt", k_) for k_ in range(8)]
                allmx = [("tmp", k_) for k_ in range(4)]
                gbuf = [GATE, GATE2]
                gres = [allpt, allmx]
                for hi in range(2):
                    hg = 2 * r + hi

                    def dst_q(tt, hi=hi):
                        return QT[0:64, hi, tt * 512:(tt + 1) * 512], ("qt", hi, tt), 0, 64

                    def extra(tt, b, hi=hi):
                        A("dve", lambda e, b=b, tt=tt, hi=hi: e.tensor_copy(out=gbuf[hi][64:65, tt * 512:(tt + 1) * 512], in_=PS[b][64:65, :]),
                          reads=[("ps", b)], writes=gres[hi])

                    proj_fm(sl, hi * 65, 65, dst_q, extra)
                for hi in range(2):
                    hg = 2 * r + hi
                    G = gbuf[hi]
                    gr = gres[hi]
                    A("act", lambda e, hg=hg, l=l, G=G: e.activation(out=G[64:65, :], in_=G[64:65, :], func=AF.Sigmoid,
                                                                      bias=BFOX[64:65, l * 4 + hg:l * 4 + hg + 1]),
                      reads=gr + [("c", "bfox")], writes=gr)
                    A("act", lambda e, G=G: e.activation(out=G[64:65, :], in_=G[64:65, :], func=AF.Ln),
                      reads=gr, writes=gr)
                    A("dve", lambda e, G=G: e.tensor_tensor_scan(out=G[64:65, :], data0=ONESF[64:65, 0:1].to_broadcast([1, S]),
                                                                  data1=G[64:65, :], initial=0.0, op0=ALU.mult, op1=ALU.add),
                      reads=gr + [("c", "onesf")], writes=gr)
                    A("dve", lambda e, hi=hi, G=G: e.tensor_scalar(out=QT[64:65, hi, :], in0=G[64:65, :], scalar1=8.0, scalar2=None, op0=ALU.mult),
                      reads=gr, writes=[("qaug", hi)])

                def gate_part_b():
                    for hi in range(2):
                        G = gbuf[hi]
                        bn = bank("s")
                        for kb in range(16):
                            mm(PS[bn][:, kb:kb + 1], G[64:65, kb * 128:(kb + 1) * 128], NEGONE[64:65, 0:1], True, True,
                               reads=gres[hi] + [("c", "negone"), ("ps", bn)], writes=[("ps", bn)])
                        A("dve", lambda e, bn=bn, hi=hi: e.tensor_copy(out=NEGCUM[:, hi, :], in_=PS[bn][:, 0:16]),
                          reads=[("ps", bn)], writes=[("negcum", hi)])
                for hi in range(2):
                    def dst_k(tt, hi=hi):
                        return KT[0:64, hi, tt * 512:(tt + 1) * 512], ("kt", hi, tt), 0, 64
                    proj_fm(sl, 130 + hi * 64, 64, dst_k)
                def evac_v(tb0, ntb, b):
                    src = PS[b][:, :].rearrange("p (t c) -> p t c", t=ntb)
                    A("dve", lambda e, src=src, tb0=tb0, ntb=ntb: e.tensor_copy(out=VT[:, tb0:tb0 + ntb, 0:64], in_=src[:, :, 0:64]),
                      reads=[("ps", b)], writes=[("vt", tb0 // 4)])
                    A("dve", lambda e, src=src, tb0=tb0, ntb=ntb: e.tensor_copy(out=VT[:, tb0:tb0 + ntb, 129:193], in_=src[:, :, 64:128]),
                      reads=[("ps", b)], writes=[("vt", tb0 // 4)])
                proj_tm(sl, 258, 128, evac_v)
                gate_part_b()
                sl_next = load_round_weights(l, r + 1)
                steps = []
                for qt in range(NT):
                    obs = [bank("a"), bank("a")]
                    ts = slice(qt * 512, (qt + 1) * 512)
                    for hi in range(2):
                        ob = obs[hi]
                        if hi == 0:
                            pv = [(ob, 65, lambda kb: VT[:, kb, 0:65])]
                            fin = None
                        else:
                            pv = [(ob, 128, lambda kb: VT[:, kb, 65:193])]

                            def fin(obs=obs, ts=ts, qt=qt, sl=sl, r=r):
                                return norm_even_odd(obs[0], obs[1], 512,
                                              lambda f, ts=ts, r=r: f(MIXT[0:64, r, ts], 0, 512),
                                              lambda f, ts=ts, r=r: f(MIXT[64:128, r, ts], 0, 512),
                                              [("mixt", r, qt)])
                        steps += causal_steps(
                            qt,
                            k_ap=lambda kb, hi=hi: KT[0:65, hi, kb * 128:(kb + 1) * 128],
                            q_ap=lambda c0, hi=hi, qt=qt: QT[0:65, hi, qt * 512 + c0:(qt + 1) * 512],
                            krows=65, tab_ap=CMF[:, :], tab_res=("c", "cmf"),
                            bias_fn=lambda kb, hi=hi: (NEGCUM[:, hi, kb:kb + 1], ("negcum", hi)),
                            pv_list=pv, qres=("qt", hi, qt), kres_fn=lambda kb, hi=hi: ("kt", hi, kb // 4),
                            vres_fn=lambda kb, hi=hi: ("vt", kb // 4), xreads=[("qaug", hi), ("kaug",)], fin=fin)
                run_steps(steps, DEFER=6)
                if r == 1:
                    for tt in range(NT):
                        out_proj_tt(sl, 2, tt)

            sl = sl_next
            for h in range(2):
                proj_fm(sl, h * 128, 128, lambda tt, h=h: (QT[:, h, tt * 512:(tt + 1) * 512], ("qt", h, tt), 0, 128))
                proj_fm(sl, 256 + h * 128, 128, lambda tt, h=h: (KT[:, h, tt * 512:(tt + 1) * 512], ("kt", h, tt), 0, 128))

            def evac_vd(tb0, ntb, b):
                src = PS[b][:, :].rearrange("p (t c) -> p t c", t=ntb)
                A("dve", lambda e, src=src, tb0=tb0, ntb=ntb: e.tensor_copy(out=VT[:, tb0:tb0 + ntb, 0:256], in_=src),
                  reads=[("ps", b)], writes=[("vt", tb0 // 4)])
            proj_tm(sl, 512, 256, evac_vd)
            sl_next = load_round_weights(l, 3)
            steps = []
            for qt in range(NT):
                for h in range(2):
                    ts = slice(qt * 512, (qt + 1) * 512)
                    for cc in range(2):
                        ob = bank("a")
                        osum = bank("a")
                        pv = [(ob, 128, lambda kb, h=h: VT[:, kb, h * 128:(h + 1) * 128]),
                              (osum, 128, lambda kb: ONESB[:, :])]

                        def fin(ob=ob, osum=osum, cc=cc, h=h, qt=qt, ts=ts, sl=sl):
                            A("dve", lambda e, osum=osum: e.reciprocal(out=TMP[:, 2, :], in_=PS[osum][:, :]),
                              reads=[("ps", osum)], writes=[("tmp", 2)])
                            A("dve", lambda e, ob=ob, cc=cc: e.tensor_tensor(out=TMP[:, 3 + cc, :], in0=PS[ob][:, :], in1=TMP[:, 2, :], op=ALU.mult),
                              reads=[("ps", ob), ("tmp", 2)], writes=[("tmp", 3 + cc)])
                            if cc == 0:
                                return None
                            A("dve", lambda e: e.scalar_tensor_tensor(out=TMP[:, 5, :], in0=TMP[:, 4, :], scalar=LAMS[:, 5:6], in1=TMP[:, 3, :],
                                                                     op0=ALU.mult, op1=ALU.add),
                              reads=[("tmp", 3), ("tmp", 4), ("neglam",)], writes=[("tmp", 5)])

                            def part2(h=h, ts=ts, qt=qt):
                                A("dve", lambda e: e.tensor_tensor(out=SQB[:, 0, :], in0=TMP[:, 5, :], in1=TMP[:, 5, :], op=ALU.mult),
                                  reads=[("tmp", 5)], writes=[("sqb", 0)])
                                b = bank("m")
                                mm(PS[b][:, :], ONESB[:, :], SQB[:, 0, :], True, True,
                                   reads=[("sqb", 0), ("c", "onesb"), ("ps", b)], writes=[("ps", b)])
                                A("act", lambda e, b=b: e.activation(out=TMP[:, 6, :], in_=PS[b][:, :], func=AF.Sqrt, bias=EPS, scale=1.0 / 128),
                                  reads=[("ps", b)], writes=[("tmp", 6)])

                                def part3(h=h, ts=ts, qt=qt):
                                    A("dve", lambda e: e.reciprocal(out=TMP[:, 7, :], in_=TMP[:, 6, :]),
                                      reads=[("tmp", 6)], writes=[("tmp", 7)])
                                    A("dve", lambda e, h=h, ts=ts: e.scalar_tensor_tensor(out=MIXT[:, h, ts], in0=TMP[:, 5, :], scalar=GDS[:, 0:1], in1=TMP[:, 7, :],
                                                                                         op0=ALU.mult, op1=ALU.mult),
                                      reads=[("tmp", 5), ("tmp", 7), ("gds",)], writes=[("mixt", h, qt)])
                                    return None
                                return part3
                            return part2
                        steps += causal_steps(
                            qt,
                            k_ap=lambda kb, h=h, cc=cc: KT[cc * 64:(cc + 1) * 64, h, kb * 128:(kb + 1) * 128],
                            q_ap=lambda c0, h=h, cc=cc, qt=qt: QT[cc * 64:(cc + 1) * 64, h, qt * 512 + c0:(qt + 1) * 512],
                            krows=64, tab_ap=TDT[:, h, :], tab_res=("c", "td"),
                            bias_fn=lambda kb, h=h, qt=qt: (BDT[:, h * 16 + kb - 4 * qt + 12:h * 16 + kb - 4 * qt + 13], ("c", "bd")),
                            pv_list=pv, qres=("qt", h, qt), kres_fn=lambda kb, h=h: ("kt", h, kb // 4),
                            vres_fn=lambda kb: ("vt", kb // 4), fin=fin)
            run_steps(steps, DEFER=3)
            for tt in range(NT):
                out_proj_tt(sl, 2, tt)

            sl = sl_next
            init_aug(2, False)
            for p in range(2):
                proj_fm(sl, p * 128, 128, lambda tt, p=p: (QT[:, p, tt * 512:(tt + 1) * 512], ("qt", p, tt), 0, 128))
                proj_fm(sl, 256 + p * 128, 128, lambda tt, p=p: (KT[:, p, tt * 512:(tt + 1) * 512], ("kt", p, tt), 0, 128))

            def evac_vc(tb0, ntb, b):
                src = PS[b][:, :].rearrange("p (t c) -> p t c", t=ntb)
                for p in range(2):
                    A("dve", lambda e, src=src, tb0=tb0, ntb=ntb, p=p: e.tensor_copy(
                        out=VT[:, tb0:tb0 + ntb, p * 193:p * 193 + 64], in_=src[:, :, p * 128:p * 128 + 64]),
                      reads=[("ps", b)], writes=[("vt", tb0 // 4)])
                    A("dve", lambda e, src=src, tb0=tb0, ntb=ntb, p=p: e.tensor_copy(
                        out=VT[:, tb0:tb0 + ntb, p * 193 + 129:p * 193 + 193], in_=src[:, :, p * 128 + 64:p * 128 + 128]),
                      reads=[("ps", b)], writes=[("vt", tb0 // 4)])
            proj_tm(sl, 512, 256, evac_vc)
            steps = []
            for qb in range(16):
                js = [j for j in range(5) if qb - 4 + j >= 0]
                pts = {}
                for j in js:
                    kb = qb - 4 + j
                    st = {}

                    def A_(j=j, kb=kb, qb=qb, st=st):
                        b = bank("q")
                        st["b"] = b
                        for h in range(4):
                            p, hh = h // 2, h % 2
                            mm(PS[b][:, h * 128:(h + 1) * 128], KT[hh * 64:(hh + 1) * 64, p, kb * 128:(kb + 1) * 128],
                               QT[hh * 64:(hh + 1) * 64, p, qb * 128:(qb + 1) * 128], True, False,
                               reads=[("kt", p, kb // 4), ("qt", p, qb // 4), ("ps", b)], writes=[("ps", b)])
                            mm(PS[b][:, h * 128:(h + 1) * 128], IDENT[:, :], TC[:, j, h * 128:(h + 1) * 128], False, True,
                               reads=[("c", "ident"), ("tc", j), ("ps", b)], writes=[("ps", b)])

                    def B_(j=j, st=st, pts=pts):
                        b = st["b"]
                        pp = next_pt()
                        pts[j] = pp
                        A("act", lambda e, b=b, pp=pp: e.activation(out=PT[:, pp, :], in_=PS[b][:, :], func=AF.Exp, scale=0.125),
                          reads=[("ps", b)], writes=[("pt", pp)])
                    steps.append([A_, B_, None, None])

                st_c = {}

                def C_(qb=qb, js=js, pts=pts, sl=sl, st_c=st_c):
                    oe = bank("a")
                    oo = bank("a")
                    for p in range(2):
                        for n, j in enumerate(js):
                            kb = qb - 4 + j
                            mm(PS[oe][0:65, p * 128:(p + 1) * 128], VT[:, kb, p * 193:p * 193 + 65], PT[:, pts[j], (2 * p) * 128:(2 * p + 1) * 128],
                               n == 0, n == len(js) - 1,
                               reads=[("pt", pts[j]), ("vt", kb // 4), ("ps", oe)], writes=[("ps", oe)])
                        for n, j in enumerate(js):
                            kb = qb - 4 + j
                            mm(PS[oo][:, p * 128:(p + 1) * 128], VT[:, kb, p * 193 + 65:p * 193 + 193], PT[:, pts[j], (2 * p + 1) * 128:(2 * p + 2) * 128],
                               n == 0, n == len(js) - 1,
                               reads=[("pt", pts[j]), ("vt", kb // 4), ("ps", oo)], writes=[("ps", oo)])

                    def dst_e(f, qb=qb):
                        for p in range(2):
                            f(MIXT[0:64, p, qb * 128:(qb + 1) * 128], p * 128, (p + 1) * 128)

                    def dst_o(f, qb=qb):
                        for p in range(2):
                            f(MIXT[64:128, p, qb * 128:(qb + 1) * 128], p * 128, (p + 1) * 128)
                    st_c["cont"] = norm_even_odd(oe, oo, 256, dst_e, dst_o, [("mixt", 0, qb // 4), ("mixt", 1, qb // 4)])
                steps[-1][2] = C_
                steps[-1][3] = (lambda st_c=st_c: st_c["cont"])
            run_steps(steps)
            out_proj_tt(sl, 2, 0)
            for tt in range(NT):
                if tt + 1 < NT:
                    out_proj_tt(sl, 2, tt + 1)
                rmsnorm(l * 16 + 8, "h", tts=[tt])
            sc.barrier(markers())

            wd = [0]

            def load_wup(fc_):
                wsl_ = fc_ % 3
                dma("pool", WUP[:, wsl_, :, :], wup[l, :, fc_ % NFC, :, :], "wu%d" % wsl_, writes=[("wup", wsl_)])

            def load_wdn(g_, dc_):
                dsl_ = wd[0] % 4
                wd[0] += 1
                dma("pool", WDN[:, dsl_, :, :], wdn[l, :, g_, dc_, :, :], "wd%d" % dsl_, writes=[("wdn", dsl_)])
                return dsl_

            def ybuf(par, sub, t2):
                return YB[:, sub, t2, :] if par == 0 else TMP[:, sub * 2 + t2, :]

            def silbuf(par, t2):
                return SIL[:, t2, :] if par == 0 else TMP[:, 4 + t2, :]

            nup = 2 * NFC
            for sub_ in range(2):
                for usl_ in range(2):
                    A("dve", lambda e, sub_=sub_, usl_=usl_: e.memset(UVR[:, sub_, usl_, 0:2], 0.0),
                      writes=[("uvr", sub_, usl_, "h")])
            load_wup(0)
            load_wup(1)
            dslots_all = {}

            def up_fc(it):
                hh = it // NFC
                fc = it % NFC
                g = fc // 11
                fi = fc % 11
                G = it // 11
                asl = it % 12
                wsl = it % 3
                if it + 2 < nup:
                    load_wup(it + 2)
                if fi >= 7:
                    dslots_all[(G, fi - 7)] = load_wdn(g, fi - 7)
                usl = fc % 2
                par = fc % 2
                for sub in range(2):
                    cp = sub * 22 + fc
                    if hh == 1:
                        if sub == 0:
                            A("dve", lambda e, sub=sub, usl=usl, cp=cp: e.tensor_copy(out=UVR[:, sub, usl, 0:2], in_=HALO[:, cp, :]),
                              reads=[("halo", cp)], writes=[("uvr", sub, usl, "h")])
                        else:
                            A("act", lambda e, sub=sub, usl=usl, cp=cp: e.activation(out=UVR[:, sub, usl, 0:2], in_=HALO[:, cp, :], func=AF.Copy),
                              reads=[("halo", cp)], writes=[("uvr", sub, usl, "h")])
                    bks = []
                    for t2 in range(2):
                        tt = hh * 2 + t2
                        ts = slice(tt * 512, (tt + 1) * 512)
                        b = bank("u")
                        bks.append(b)
                        for c in range(KC):
                            mm(PS[b][:, :], WUP[:, wsl, c, sub * 128:(sub + 1) * 128], hT[:, c, ts], c == 0, c == KC - 1,
                               reads=[("wup", wsl), ("hT", c, tt), ("ps", b)], writes=[("ps", b)])
                    for t2 in range(2):
                        b = bks[t2]
                        lo = 2 + t2 * 512
                        A("act", lambda e, b=b, sub=sub, usl=usl, lo=lo: e.activation(out=UVR[:, sub, usl, lo:lo + 512], in_=PS[b][:, :], func=AF.Copy),
                          reads=[("ps", b)], writes=[("uvr", sub, usl, t2)])
                    for t2 in range(2):
                        lo = 2 + t2 * 512
                        ysl = t2
                        A("act", lambda e, sub=sub, usl=usl, lo=lo, ysl=ysl, cp=cp, l=l, par=par: e.activation(
                            out=ybuf(par, sub, ysl), in_=UVR[:, sub, usl, lo - 2:lo + 510], func=AF.Identity,
                            scale=CONVP[:, l, cp, 0:1], bias=CONVP[:, l, cp, 3:4]),
                          reads=[("uvr", sub, usl, 0), ("uvr", sub, usl, 1), ("uvr", sub, usl, "h"), ("c", "convp")], writes=[("yb", par, sub, ysl)])
                    for t2 in range(2):
                        lo = 2 + t2 * 512
                        ysl = t2
                        A("dve", lambda e, sub=sub, usl=usl, lo=lo, ysl=ysl, cp=cp, l=l, par=par: e.scalar_tensor_tensor(
                            out=ybuf(par, sub, ysl), in0=UVR[:, sub, usl, lo - 1:lo + 511], scalar=CONVP[:, l, cp, 1:2], in1=ybuf(par, sub, ysl),
                            op0=ALU.mult, op1=ALU.add),
                          reads=[("uvr", sub, usl, 0), ("uvr", sub, usl, 1), ("uvr", sub, usl, "h"), ("yb", par, sub, ysl), ("c", "convp")], writes=[("yb", par, sub, ysl)])
                    for t2 in range(2):
                        b = bks[t2]
                        ysl = t2
                        A("dve", lambda e, b=b, sub=sub, ysl=ysl, cp=cp, l=l, par=par: e.scalar_tensor_tensor(
                            out=ybuf(par, sub, ysl), in0=PS[b][:, :], scalar=CONVP[:, l, cp, 2:3], in1=ybuf(par, sub, ysl),
                            op0=ALU.mult, op1=ALU.add),
                          reads=[("ps", b), ("yb", par, sub, ysl), ("c", "convp")], writes=[("yb", par, sub, ysl)])
                    if hh == 0:
                        A("act", lambda e, sub=sub, usl=usl, cp=cp: e.activation(out=HALO[:, cp, :], in_=UVR[:, sub, usl, 1024:1026], func=AF.Copy),
                          reads=[("uvr", sub, usl, 1)], writes=[("halo", cp)])
                for t2 in range(2):
                    A("act", lambda e, t2=t2, par=par: e.activation(out=silbuf(par, t2), in_=ybuf(par, 1, t2), func=AF.Silu),
                      reads=[("yb", par, 1, t2)], writes=[("sil", par, t2)])
                    A("pool" if POOL_FFN else "dve", lambda e, t2=t2, asl=asl, par=par: e.tensor_tensor(out=ACTT[:, asl, t2 * 512:(t2 + 1) * 512], in0=silbuf(par, t2), in1=ybuf(par, 0, t2), op=ALU.mult),
                      reads=[("sil", par, t2), ("yb", par, 0, t2)], writes=[("actt", asl, t2)])

            def down(G):
                hh = G // 2
                g = G % 2
                for dc in range(KC):
                    dsl = dslots_all[(G, dc)]
                    for t2 in range(2):
                        tt = hh * 2 + t2
                        ts = slice(tt * 512, (tt + 1) * 512)
                        b = bank("d")
                        for fi in range(11):
                            asl = (G * 11 + fi) % 12
                            mm(PS[b][:, :], WDN[:, dsl, fi, :], ACTT[:, asl, t2 * 512:(t2 + 1) * 512], fi == 0, fi == 10,
                               reads=[("wdn", dsl), ("actt", asl, t2), ("ps", b)], writes=[("ps", b)])
                        A("dve", lambda e, b=b, dc=dc, ts=ts: e.tensor_tensor(out=xT[:, dc, ts], in0=PS[b][:, :], in1=xT[:, dc, ts], op=ALU.add),
                          reads=[("ps", b), ("xT", dc, tt)], writes=[("xT", dc, tt)])
                    if dc + 4 < KC:
                        dslots_all[(G, dc + 4)] = load_wdn(g, dc + 4)

            done_up = set()
            for G in range(4):
                for fi in range(11):
                    it = G * 11 + fi
                    if it not in done_up:
                        up_fc(it)
                        done_up.add(it)
                if G < 3:
                    up_fc((G + 1) * 11)
                    done_up.add((G + 1) * 11)
                down(G)
            sc.barrier(markers())
        for tt in range(NT):
            if final_norm:
                rmsnorm(32, "x", tts=[tt])
            dma("sp", out_t[s, :, :, tt * 512:(tt + 1) * 512], xT[:, :, tt * 512:(tt + 1) * 512], "o%d" % tt,
                reads=[("xT", c, tt) for c in range(KC)])

    sc.emit(final_lanes=["o%d" % tt for tt in range(NT)])
    return nc


def _prep_inputs(inputs):
    f = lambda a: np.ascontiguousarray(a, dtype=np.float32)
    perm = _win_perm()
    cst = _host_consts()
    w_in = np.asarray(inputs["w_in"])
    win = f(w_in[:, :, perm].reshape(2, KC, 128, 2308).transpose(0, 2, 1, 3))
    wout = f(np.asarray(inputs["w_out"]).reshape(2, 6, 128, D).transpose(0, 2, 1, 3))
    wu = np.asarray(inputs["w_ffn_in"]).reshape(2, KC, 128, 2, NFC, 128)
    wup = f(wu.transpose(0, 2, 4, 1, 3, 5).reshape(2, 128, NFC, KC, 256))
    wd = np.asarray(inputs["w_ffn_out"]).reshape(2, 2, 11, 128, 8, 128)
    wdn = f(wd.transpose(0, 3, 1, 4, 2, 5))
    gs = [np.asarray(inputs["g_mix"])[0], np.asarray(inputs["g_ffn"])[0], np.asarray(inputs["g_mix"])[1],
          np.asarray(inputs["g_ffn"])[1], np.asarray(inputs["g_final"])]
    gcols = f(np.concatenate([g.reshape(KC, 128).T for g in gs], axis=1))
    cw = np.asarray(inputs["conv_w"]).reshape(2, 3, 44, 128)
    cb = np.asarray(inputs["conv_b"]).reshape(2, 1, 44, 128)
    convp = f(np.concatenate([cw, cb], axis=1).transpose(3, 0, 2, 1))
    gdiff = f(np.asarray(inputs["g_diff"]).T)
    bfox = f(np.asarray(inputs["b_fox_f"]).reshape(1, 8))
    dlam = f(np.broadcast_to(np.asarray(inputs["diff_lambda"]).reshape(1, 2, 256), (128, 2, 256)))
    rel = np.asarray(inputs["rel_bias"])
    relg = f(rel[:, :, cst["relidx"]].transpose(0, 3, 2, 1, 4))
    shared = dict(win=win, wout=wout, wup=wup, wdn=wdn, gcols=gcols, convp=convp, gdiff=gdiff, bfox=bfox,
                  dlam=dlam, relg=relg, relmask=f(cst["relmask"]), ident=f(cst["ident"]), cmf=f(cst["cmf"]),
                  td=f(cst["td"]), bd=f(cst["bd"]))
    return shared


def _x_shard(x, b0, n):
    xs = np.asarray(x[b0:b0 + n], dtype=np.float32)
    return np.ascontiguousarray(xs.transpose(0, 2, 1).reshape(n, KC, 128, S).transpose(0, 2, 1, 3))


_PROG = {}


def kernel(**inputs):
    x = np.asarray(inputs["x"])
    B = x.shape[0]
    ncores = 8
    per = B // ncores
    shared = _prep_inputs(inputs)
    key = (per, (0, 1), True)
    if key not in _PROG:
        _PROG[key] = build_program(per, (0, 1), True)
    nc = _PROG[key]
    in_maps = []
    for c in range(ncores):
        m = dict(shared)
        m["xT_in"] = _x_shard(x, c * per, per)
        in_maps.append(m)
    res = run_bass_kernel_spmd(nc, in_maps, core_ids=list(range(ncores)))
    out = np.empty((B, S, D), np.float32)
    for c in range(ncores):
        o = res.results[c]["outT"]
        out[c * per:(c + 1) * per] = o.transpose(0, 2, 1, 3).reshape(per, D, S).transpose(0, 2, 1)
    return out
```

```python
import math
import numpy as np
import concourse.bass as bass
import concourse.mybir as mybir
from concourse.bass_utils import run_bass_kernel_spmd

F32 = mybir.dt.float32
BF16 = mybir.dt.bfloat16
AF = mybir.ActivationFunctionType
ALU = mybir.AluOpType

POOL_FFN = False
S = 2048
NT = 4
D = 1024
KC = 8
NFC = 22
NEG = -30000.0
EPS = 1e-6
SLOPES = [2.0 ** (-8.0 * (i + 1) / 2) for i in range(2)]


class Op:
    __slots__ = ("eng", "fn", "deps", "sig", "count", "lane", "lane_val", "is_dma")


class Sched:
    ENGS = ("pe", "act", "dve", "pool", "sp")

    def __init__(self, nc):
        self.nc = nc
        self.ops = []
        self.last_write = {}
        self.readers = {}
        self.lanes = {}
        self.lane_cnt = {}
        self.pending_dma = []
        self.bar_deps = []
        self.need_bar = {e: False for e in self.ENGS}

    def add(self, eng, fn, reads=(), writes=(), lane=None):
        i = len(self.ops)
        op = Op()
        op.eng = eng
        op.fn = fn
        op.sig = False
        op.count = 0
        op.is_dma = lane is not None
        op.lane = lane
        op.lane_val = 0
        deps = {}
        for r in reads:
            lw = self.last_write.get(r)
            if lw is not None:
                deps[lw] = "raw"
        for w in writes:
            lw = self.last_write.get(w)
            if lw is not None and lw not in deps:
                deps[lw] = "waw"
            rd = self.readers.get(w)
            if rd:
                for k, j in rd.items():
                    if isinstance(j, list):
                        for jj in j:
                            if jj not in deps:
                                deps[jj] = "war"
                    elif j not in deps:
                        deps[j] = "war"
        if self.need_bar[eng]:
            for j in self.bar_deps:
                deps[j] = "raw"
            self.need_bar[eng] = False
        for r in reads:
            d = self.readers.setdefault(r, {})
            if op.is_dma:
                d.setdefault("dma", []).append(i)
            else:
                d[eng] = i
        for w in writes:
            self.last_write[w] = i
            self.readers[w] = {}
        if op.is_dma:
            self.lane_cnt[lane] = self.lane_cnt.get(lane, 0) + 1
            op.lane_val = 16 * self.lane_cnt[lane]
            self.pending_dma.append(i)
        op.deps = deps
        self.ops.append(op)
        return i

    def barrier(self, markers):
        self.bar_deps = list(markers) + list(self.pending_dma)
        self.pending_dma = []
        for e in self.ENGS:
            self.need_bar[e] = True

    def _needs_sync(self, a, b, kind):
        if a.is_dma or b.is_dma:
            return True
        if a.eng != b.eng:
            return True
        if a.eng != "pe":
            return True
        return False

    def finalize(self):
        ops = self.ops
        for b in ops:
            for j, kind in b.deps.items():
                a = ops[j]
                if self._needs_sync(a, b, kind) and not a.is_dma:
                    a.sig = True
        cnt = {e: 0 for e in self.ENGS}
        for o in ops:
            if o.sig:
                cnt[o.eng] += 1
                o.count = cnt[o.eng]

    def emit(self, final_lanes):
        nc = self.nc
        self.finalize()
        ops = self.ops
        sems = {e: nc.alloc_semaphore("sem_" + e) for e in self.ENGS}
        lane_sems = {l: nc.alloc_semaphore("lane_" + l) for l in self.lane_cnt}
        per_eng = {e: [] for e in self.ENGS}
        for i, o in enumerate(ops):
            per_eng[o.eng].append(i)

        def replay(ename, e):
            waited = {}
            for i in per_eng[ename]:
                o = ops[i]
                need = {}
                for j, kind in o.deps.items():
                    a = ops[j]
                    if not self._needs_sync(a, o, kind):
                        continue
                    if a.is_dma:
                        key, val = ("L", a.lane), a.lane_val
                    else:
                        key, val = ("E", a.eng), a.count
                    if val > need.get(key, 0):
                        need[key] = val
                for key, val in need.items():
                    if val > waited.get(key, 0):
                        waited[key] = val
                        sem = lane_sems[key[1]] if key[0] == "L" else sems[key[1]]
                        e.wait_ge(sem, val)
                inst = o.fn(e)
                if o.is_dma:
                    inst.then_inc(lane_sems[o.lane], 16)
                elif o.sig:
                    inst.then_inc(sems[ename], 1)
            if ename == "sp":
                for l in final_lanes:
                    e.wait_ge(lane_sems[l], 16 * self.lane_cnt[l])

        with nc.Block() as block:
            @block.tensor
            def _(e):
                replay("pe", e)

            @block.scalar
            def _(e):
                replay("act", e)

            @block.vector
            def _(e):
                replay("dve", e)

            @block.gpsimd
            def _(e):
                replay("pool", e)

            @block.sync
            def _(e):
                replay("sp", e)


def _win_perm():
    cols = []
    for r in range(2):
        a, b = 2 * r, 2 * r + 1
        cols += list(range(a * 64, a * 64 + 64)) + [768 + a]
        cols += list(range(b * 64, b * 64 + 64)) + [768 + b]
        cols += list(range(256 + a * 64, 256 + a * 64 + 64))
        cols += list(range(256 + b * 64, 256 + b * 64 + 64))
        cols += list(range(512 + a * 64, 512 + a * 64 + 128))
    base = 772
    cols += list(range(base, base + 768))
    cols += list(range(base + 768, base + 1536))
    assert len(cols) == 2308 and len(set(cols)) == 2308
    return np.asarray(cols)


ROUND_COL0 = [0, 386, 772, 1540]
ROUND_NCOL = [386, 386, 768, 768]


def _host_consts():
    k = np.arange(128)[:, None]
    q = np.arange(128)[None, :]
    c = {}
    c["ident"] = np.eye(128, dtype=np.float32)
    c["cmf"] = np.where(q >= k, 0.0, 8 * NEG).astype(np.float32)
    td = np.zeros((128, 2, 128), np.float32)
    for h in range(2):
        t = np.where(q >= k, 0.0, -2.0 * SLOPES[h] * (k - q) * 8.0)
        t = np.where((k // 64) > (q // 64), 8 * NEG, t)
        td[:, h, :] = t
    c["td"] = td
    bd = np.zeros((128, 32), np.float32)
    for h in range(2):
        for jj in range(16):
            bd[:, h * 16 + jj] = SLOPES[h] * (np.arange(128) + 128.0 * (jj - 12))
    c["bd"] = bd
    idx = np.zeros((5, 128, 128), np.int64)
    msk = np.zeros((128, 5, 4, 128), np.float32)
    for j in range(5):
        tms = 128 * (4 - j) + q - k
        idx[j] = np.clip(tms, -63, 128) + 63
        qc = q // 64
        kc = k // 64 + 2 * (j - 4)
        valid = (kc <= qc) & (kc >= qc - 8)
        msk[:, j, :, :] = np.where(valid, 0.0, 8 * NEG)[:, None, :]
    c["relidx"] = idx
    c["relmask"] = msk
    return c


def build_program(n_seq=2, layers=(0, 1), final_norm=True):
    nc = bass.Bass("TRN2", target_bir_lowering=False)
    dt_in = lambda name, shape: nc.dram_tensor(name, list(shape), F32, kind="ExternalInput").ap()
    x_in = dt_in("xT_in", (n_seq, 128, KC, S))
    win = dt_in("win", (2, 128, KC, 2308))
    wout = dt_in("wout", (2, 128, 6, D))
    wup = dt_in("wup", (2, 128, NFC, KC, 256))
    wdn = dt_in("wdn", (2, 128, 2, 8, 11, 128))
    gcols_in = dt_in("gcols", (128, 40))
    convp_in = dt_in("convp", (128, 2, 44, 4))
    gdiff_in = dt_in("gdiff", (128, 2))
    bfox_in = dt_in("bfox", (1, 8))
    dlam_in = dt_in("dlam", (128, 2, 256))
    relg_in = dt_in("relg", (2, 128, 5, 4, 128))
    relmask_in = dt_in("relmask", (128, 5, 4, 128))
    ident_in = dt_in("ident", (128, 128))
    cmf_in = dt_in("cmf", (128, 128))
    td_in = dt_in("td", (128, 2, 128))
    bd_in = dt_in("bd", (128, 32))
    out_t = nc.dram_tensor("outT", [n_seq, 128, KC, S], F32, kind="ExternalOutput").ap()

    sc = Sched(nc)
    off = [(nc.sbuf_base + 31) // 32 * 32]

    def sb(name, shape, dt, at=None):
        nbytes = int(np.prod(shape[1:])) * (4 if dt == F32 else 2)
        if at is None:
            o = off[0]
            off[0] += (nbytes + 31) // 32 * 32
        else:
            o = at
        return nc.alloc_sbuf_tensor_at(name, list(shape), dt, offset=o)

    xT = sb("xT", [128, KC, S], F32)
    hT = sb("hT", [128, KC, S], BF16)
    r0 = off[0]
    QT = sb("QT", [128, 2, S], BF16)
    KT = sb("KT", [128, 2, S], BF16)
    VT = sb("VT", [128, 16, 386], BF16)
    mixt_off = off[0]
    MIXT = sb("MIXT", [128, 2, S], BF16)
    WPART = sb("WPART", [128, 2, KC, 768], BF16)
    WOPART = sb("WOPART", [128, 2, 2, D], BF16)
    pt_off = off[0]
    PT = sb("PT", [128, 8, 512], BF16)
    r_end_attn = off[0]
    GATE = sb("GATE", [128, S], F32, at=pt_off)
    off[0] = r0
    ACTT = sb("ACTT", [128, 12, 1024], BF16)
    WUP = sb("WUP", [128, 3, KC, 256], BF16)
    WDN = sb("WDN", [128, 4, 11, 128], BF16)
    UVR = sb("UVR", [128, 2, 2, 1032], F32)
    YB = sb("YB", [128, 2, 2, 512], F32)
    SIL = sb("SIL", [128, 2, 512], F32)
    assert off[0] <= r_end_attn, (off[0], r_end_attn)
    off[0] = r_end_attn
    tmp_off = off[0]
    TMP = sb("TMP", [128, 8, 512], F32)
    GATE2 = sb("GATE2", [128, S], F32, at=tmp_off)
    TMPB = TMP
    SQB = sb("SQB", [128, 2, 512], BF16)
    IDENT = sb("IDENT", [128, 128], BF16)
    ONESB = sb("ONESB", [128, 128], BF16)
    ONESF = sb("ONESF", [128, 128], F32)
    NEGONE = sb("NEGONE", [128, 2], F32)
    CMF = sb("CMF", [128, 128], BF16)
    TDT = sb("TDT", [128, 2, 128], BF16)
    BDT = sb("BDT", [128, 32], F32)
    TC = sb("TC", [128, 5, 512], BF16)
    TCS = sb("TCS", [128, 2, 512], F32)
    GCOLS = sb("GCOLS", [128, 40], F32)
    CONVP = sb("CONVP", [128, 2, 44, 4], F32)
    GDIFF = sb("GDIFF", [128, 2], F32)
    GDS = sb("GDS", [128, 2], F32)
    BFOX = sb("BFOX", [128, 8], F32)
    DLAM = sb("DLAM", [128, 2, 256], F32)
    LAMT = sb("LAMT", [128, 2, 128], F32)
    LAMS = sb("LAMS", [128, 8], F32)
    NEGCUM = sb("NEGCUM", [128, 2, 16], F32)
    HALO = sb("HALO", [128, 44, 2], F32)
    SCR = sb("SCR", [128, 8], F32)
    assert off[0] <= nc.sbuf_base + nc.sbuf_bytes_remaining, (off[0], nc.sbuf_base, nc.sbuf_bytes_remaining)

    PS = [nc.alloc_psum_tensor("ps%d" % b, [128, 512], F32) for b in range(8)]
    ring = {"s": [0, 1, 2, 3], "a": [4, 5, 6, 7], "q": [0, 1, 2], "m": [3], "u": [0, 1, 2, 3, 4, 5], "d": [6, 7]}
    ring_pos = {k_: 0 for k_ in ring}

    def bank(r):
        b = ring[r][ring_pos[r] % len(ring[r])]
        ring_pos[r] += 1
        return b

    A = sc.add

    def dma(eng, out, in_, lane, reads=(), writes=()):
        return A(eng, lambda e, out=out, in_=in_: e.dma_start(out=out, in_=in_), reads=reads, writes=writes, lane=lane)

    dma("pool", IDENT[:, :], ident_in[:, :], "c0", writes=[("c", "ident")])
    dma("pool", CMF[:, :], cmf_in[:, :], "c1", writes=[("c", "cmf")])
    dma("pool", TDT[:, :, :], td_in[:, :, :], "c2", writes=[("c", "td")])
    dma("sp", BDT[:, :], bd_in[:, :], "c3", writes=[("c", "bd")])
    dma("sp", GCOLS[:, :], gcols_in[:, :], "c4", writes=[("c", "gcols")])
    dma("sp", CONVP[:, :, :, :], convp_in[:, :, :, :], "c5", writes=[("c", "convp")])
    dma("sp", GDIFF[:, :], gdiff_in[:, :], "c6", writes=[("c", "gdiff")])
    dma("sp", BFOX[64:65, :], bfox_in[:, :], "c7", writes=[("c", "bfox")])
    dma("sp", DLAM[:, :, :], dlam_in[:, :, :], "c8", writes=[("c", "dlam")])
    A("pool", lambda e: e.memset(ONESB[:, :], 1.0), writes=[("c", "onesb")])
    A("pool", lambda e: e.memset(ONESF[:, :], 1.0), writes=[("c", "onesf")])
    A("pool", lambda e: e.memset(NEGONE[:, :], -1.0), writes=[("c", "negone")])
    A("pool", lambda e: e.memset(SCR[:, :], 0.0), writes=[("scr", "act"), ("scr", "dve"), ("scr", "zero")])

    def markers():
        m1 = A("act", lambda e: e.activation(out=SCR[0:1, 0:1], in_=SCR[0:1, 2:3], func=AF.Copy),
               reads=[("scr", "zero")], writes=[("scr", "act")])
        m2 = A("dve", lambda e: e.tensor_copy(out=SCR[0:1, 4:5], in_=SCR[0:1, 6:7]),
               reads=[("scr", "zero")], writes=[("scr", "dve")])
        return [m1, m2]

    def mm(out, lhsT, rhs, start, stop, reads, writes):
        return A("pe", lambda e, out=out, lhsT=lhsT, rhs=rhs, start=start, stop=stop:
                 e.matmul(out, lhsT, rhs, start=start, stop=stop), reads=reads, writes=writes)

    def rmsnorm(gcol0, dst_kind, tts=None):
        for tt in (range(NT) if tts is None else tts):
            ts = slice(tt * 512, (tt + 1) * 512)
            b = bank("s")
            for c in range(KC):
                sl = c % 2
                A("act", lambda e, c=c, sl=sl, ts=ts: e.activation(out=SQB[:, sl, :], in_=xT[:, c, ts], func=AF.Square),
                  reads=[("xT", c, tt)], writes=[("sqb", sl)])
                mm(PS[b][:, :], ONESB[:, :], SQB[:, sl, :], c == 0, c == KC - 1,
                   reads=[("sqb", sl), ("c", "onesb"), ("ps", b)], writes=[("ps", b)])
            A("act", lambda e, b=b: e.activation(out=TMP[:, 0, :], in_=PS[b][:, :], func=AF.Ln, bias=EPS, scale=1.0 / D),
              reads=[("ps", b)], writes=[("tmp", 0)])
            A("act", lambda e: e.activation(out=TMP[:, 1, :], in_=TMP[:, 0, :], func=AF.Exp, scale=-0.5),
              reads=[("tmp", 0)], writes=[("tmp", 1)])
            for c in range(KC):
                if dst_kind == "h":
                    A("dve", lambda e, c=c, ts=ts: e.scalar_tensor_tensor(
                        out=hT[:, c, ts], in0=xT[:, c, ts], scalar=GCOLS[:, gcol0 + c:gcol0 + c + 1], in1=TMP[:, 1, :],
                        op0=ALU.mult, op1=ALU.mult),
                      reads=[("xT", c, tt), ("tmp", 1), ("c", "gcols")], writes=[("hT", c, tt)])
                else:
                    A("dve", lambda e, c=c, ts=ts: e.scalar_tensor_tensor(
                        out=xT[:, c, ts], in0=xT[:, c, ts], scalar=GCOLS[:, gcol0 + c:gcol0 + c + 1], in1=TMP[:, 1, :],
                        op0=ALU.mult, op1=ALU.mult),
                      reads=[("xT", c, tt), ("tmp", 1), ("c", "gcols")], writes=[("xT", c, tt)])

    wslot = [0]

    def load_round_weights(l, r):
        sl = wslot[0] % 2
        wslot[0] += 1
        n = ROUND_NCOL[r]
        c0 = ROUND_COL0[r]
        dma("pool", WPART[:, sl, :, 0:n], win[l, :, :, c0:c0 + n], "wp%d" % sl, writes=[("wpart", sl)])
        mc0 = [0, 0, 2, 4][r]
        nmc = [0, 2, 2, 2][r]
        if nmc:
            dma("pool", WOPART[:, sl, 0:nmc, :], wout[l, :, mc0:mc0 + nmc, :], "wo%d" % sl, writes=[("wopart", sl)])
        return sl

    def proj_fm(sl, col0, m, dst_fn, extra_fn=None, wreads=()):
        for tt in range(NT):
            ts = slice(tt * 512, (tt + 1) * 512)
            b = bank("s")
            for c in range(KC):
                mm(PS[b][0:m, :], WPART[:, sl, c, col0:col0 + m], hT[:, c, ts], c == 0, c == KC - 1,
                   reads=[("wpart", sl), ("hT", c, tt), ("ps", b)], writes=[("ps", b)])
            m64 = min(m, 64) if extra_fn is not None else m
            out_ap, res, p0, p1 = dst_fn(tt)
            A("dve", lambda e, b=b, out_ap=out_ap, p0=p0, p1=p1: e.tensor_copy(out=out_ap, in_=PS[b][p0:p1, :]),
              reads=[("ps", b)], writes=[res])
            if extra_fn is not None:
                extra_fn(tt, b)

    def proj_tm(sl, col0, ncol, evac):
        ntb = 512 // ncol
        for g in range(16 // ntb):
            b = bank("s")
            for i in range(ntb):
                tb = g * ntb + i
                for c in range(KC):
                    mm(PS[b][:, i * ncol:(i + 1) * ncol], hT[:, c, tb * 128:(tb + 1) * 128], WPART[:, sl, c, col0:col0 + ncol],
                       c == 0, c == KC - 1,
                       reads=[("wpart", sl), ("hT", c, tb // 4), ("ps", b)], writes=[("ps", b)])
            evac(g * ntb, ntb, b)

    def out_proj_tt(sl, nmc, tt):
        ts = slice(tt * 512, (tt + 1) * 512)
        for dc in range(KC):
            b = bank("s")
            for mc in range(nmc):
                mm(PS[b][:, :], WOPART[:, sl, mc, dc * 128:(dc + 1) * 128], MIXT[:, mc, ts], mc == 0, mc == nmc - 1,
                   reads=[("wopart", sl), ("mixt", mc, tt), ("ps", b)], writes=[("ps", b)])
            A("dve", lambda e, b=b, dc=dc, ts=ts: e.tensor_tensor(out=xT[:, dc, ts], in0=PS[b][:, :], in1=xT[:, dc, ts], op=ALU.add),
              reads=[("ps", b), ("xT", dc, tt)], writes=[("xT", dc, tt)])

    ptpos = [0]

    def next_pt():
        s_ = ptpos[0] % 8
        ptpos[0] += 1
        return s_

    LOOK = 2

    def run_steps(steps, DEFER=2):
        n = len(steps)
        pending = []
        for i in range(min(LOOK, n)):
            steps[i][0]()
        for i in range(n):
            steps[i][1]()
            if i + LOOK < n:
                steps[i + LOOK][0]()
            due_now = [p_ for p_ in pending if p_[0] <= i]
            pending = [p_ for p_ in pending if p_[0] > i]
            for (due, fn) in due_now:
                r_ = fn()
                if r_ is not None:
                    pending.append((i + DEFER, r_))
            if steps[i][2] is not None:
                steps[i][2]()
            if steps[i][3] is not None:
                cont = steps[i][3]()
                if cont is not None:
                    pending.append((i + DEFER, cont))
        while pending:
            (due, fn) = pending.pop(0)
            r_ = fn()
            if r_ is not None:
                pending.append((due, r_))

    def causal_steps(qt, k_ap, q_ap, krows, tab_ap, tab_res, bias_fn, pv_list, qres, kres_fn, vres_fn, xreads=(), fin=None):
        nkb = 4 * qt + 4
        steps = []
        for kb in range(nkb):
            i = kb - 4 * qt
            c0 = max(i, 0) * 128
            st = {}

            def A_(kb=kb, i=i, c0=c0, st=st):
                b = bank("q")
                st["b"] = b
                mm(PS[b][:, c0:512], k_ap(kb), q_ap(c0), True, i < 0,
                   reads=[kres_fn(kb), qres, ("ps", b)] + list(xreads), writes=[("ps", b)])
                if i >= 0:
                    mm(PS[b][:, c0:c0 + 128], IDENT[:, :], tab_ap, False, True,
                       reads=[("c", "ident"), tab_res, ("ps", b)], writes=[("ps", b)])

            def B_(kb=kb, c0=c0, st=st):
                b = st["b"]
                p = next_pt()
                st["p"] = p
                bias_ap, bias_res = bias_fn(kb)
                A("act", lambda e, b=b, p=p, c0=c0, bias_ap=bias_ap: e.activation(
                    out=PT[:, p, c0:512], in_=PS[b][:, c0:512], func=AF.Exp, bias=bias_ap, scale=0.125),
                  reads=[("ps", b), bias_res], writes=[("pt", p)])

            def C_(kb=kb, c0=c0, st=st):
                p = st["p"]
                for (ob, rows, lhs_fn) in pv_list:
                    mm(PS[ob][0:rows, c0:512], lhs_fn(kb), PT[:, p, c0:512], kb == 0, kb == nkb - 1,
                       reads=[("pt", p), vres_fn(kb), ("ps", ob)], writes=[("ps", ob)])

            steps.append([A_, B_, C_, None])
        steps[-1][3] = fin
        return steps


    def norm_even_odd(ob_e, ob_o, ncols, dst_e, dst_o, dres):
        A("dve", lambda e: e.reciprocal(out=TMP[64:65, 2, 0:ncols], in_=PS[ob_e][64:65, 0:ncols]),
          reads=[("ps", ob_e)], writes=[("tmp", 2)])
        A("dve", lambda e: e.reciprocal(out=TMP[0:1, 4, 0:ncols], in_=PS[ob_o][0:1, 0:ncols]),
          reads=[("ps", ob_o)], writes=[("tmp", 4)])

        def part2():
            b = bank("m")
            mm(PS[b][0:64, 0:ncols], ONESF[64:65, 0:64], TMP[64:65, 2, 0:ncols], True, True,
               reads=[("tmp", 2), ("c", "onesf"), ("ps", b)], writes=[("ps", b)])
            A("dve", lambda e, b=b: e.tensor_copy(out=TMP[0:64, 3, 0:ncols], in_=PS[b][0:64, 0:ncols]),
              reads=[("ps", b)], writes=[("tmp", 3)])
            dst_e(lambda out_ap, c_lo, c_hi: A(
                "dve", lambda e, out_ap=out_ap, c_lo=c_lo, c_hi=c_hi: e.tensor_tensor(
                    out=out_ap, in0=PS[ob_e][0:64, c_lo:c_hi], in1=TMP[0:64, 3, c_lo:c_hi], op=ALU.mult),
                reads=[("ps", ob_e), ("tmp", 3)], writes=dres))
            b2 = bank("m")
            mm(PS[b2][:, 0:ncols], ONESF[0:1, 0:128], TMP[0:1, 4, 0:ncols], True, True,
               reads=[("tmp", 4), ("c", "onesf"), ("ps", b2)], writes=[("ps", b2)])
            A("dve", lambda e, b2=b2: e.tensor_copy(out=TMP[64:128, 5, 0:ncols], in_=PS[b2][64:128, 0:ncols]),
              reads=[("ps", b2)], writes=[("tmp", 5)])
            dst_o(lambda out_ap, c_lo, c_hi: A(
                "dve", lambda e, out_ap=out_ap, c_lo=c_lo, c_hi=c_hi: e.tensor_tensor(
                    out=out_ap, in0=PS[ob_o][64:128, c_lo:c_hi], in1=TMP[64:128, 5, c_lo:c_hi], op=ALU.mult),
                reads=[("ps", ob_o), ("tmp", 5)], writes=dres))
        return part2

    def init_aug(npairs, fox):
        allkt = [("kt", i_, t_) for i_ in range(2) for t_ in range(NT)]
        allvt = [("vt", g_) for g_ in range(4)]
        if fox:
            A("dve", lambda e: e.memset(KT[64:65, :, :], 1.0), writes=allkt + [("kaug",)])
        for p in range(npairs):
            o = p * 193
            A("dve", lambda e, o=o: e.memset(VT[:, :, o + 64:o + 66], 1.0), writes=allvt)
            A("dve", lambda e, o=o: e.memset(VT[:, :, o + 66:o + 129], 0.0), writes=allvt)

    for s in range(n_seq):
        for tt in range(NT):
            dma("sp", xT[:, :, tt * 512:(tt + 1) * 512], x_in[s, :, :, tt * 512:(tt + 1) * 512], "x%d" % tt,
                writes=[("xT", c, tt) for c in range(KC)])
        for l in layers:
            lambda_init = 0.8 - 0.6 * math.exp(-0.3 * l)
            A("dve", lambda e, l=l: e.tensor_tensor(out=LAMT[:, 0, 0:64], in0=DLAM[:, l, 0:64], in1=DLAM[:, l, 64:128], op=ALU.mult),
              reads=[("c", "dlam")], writes=[("lamt", 0)])
            A("dve", lambda e, l=l: e.tensor_tensor(out=LAMT[:, 1, 0:64], in0=DLAM[:, l, 128:192], in1=DLAM[:, l, 192:256], op=ALU.mult),
              reads=[("c", "dlam")], writes=[("lamt", 1)])
            A("dve", lambda e: e.reduce_sum(out=LAMS[:, 0:1], in_=LAMT[:, 0, 0:64], axis=mybir.AxisListType.X),
              reads=[("lamt", 0)], writes=[("lams", 0)])
            A("dve", lambda e: e.reduce_sum(out=LAMS[:, 1:2], in_=LAMT[:, 1, 0:64], axis=mybir.AxisListType.X),
              reads=[("lamt", 1)], writes=[("lams", 1)])
            A("act", lambda e: e.activation(out=LAMS[:, 2:4], in_=LAMS[:, 0:2], func=AF.Exp),
              reads=[("lams", 0), ("lams", 1)], writes=[("lams", 2)])
            A("dve", lambda e: e.tensor_tensor(out=LAMS[:, 4:5], in0=LAMS[:, 3:4], in1=LAMS[:, 2:3], op=ALU.subtract),
              reads=[("lams", 2)], writes=[("lams", 4)])
            A("dve", lambda e, li=lambda_init: e.tensor_scalar(out=LAMS[:, 5:6], in0=LAMS[:, 4:5], scalar1=-li, scalar2=None, op0=ALU.add),
              reads=[("lams", 4)], writes=[("neglam",)])
            A("dve", lambda e, l=l, li=lambda_init: e.tensor_scalar(out=GDS[:, 0:1], in0=GDIFF[:, l:l + 1], scalar1=1.0 - li, scalar2=None, op0=ALU.mult),
              reads=[("c", "gdiff")], writes=[("gds",)])
            for j in range(5):
                dma("sp", TCS[:, 0, :], relg_in[l, :, j, :, :], "tc0", writes=[("tcs", 0)])
                dma("sp", TCS[:, 1, :], relmask_in[:, j, :, :], "tc1", writes=[("tcs", 1)])
                A("dve", lambda e, j=j: e.scalar_tensor_tensor(
                    out=TC[:, j, :], in0=TCS[:, 0, :], scalar=8.0, in1=TCS[:, 1, :], op0=ALU.mult, op1=ALU.add),
                  reads=[("tcs", 0), ("tcs", 1)], writes=[("tc", j)])

            rmsnorm(l * 16, "h")

            for r in range(2):
                if r == 0:
                    sl_next = load_round_weights(l, 0)
                sl = sl_next
                init_aug(1, True)
                allpt = [("pt", k_) for k_ in range(8)]
                allmx = [("tmp", k_) for k_ in range(4)]
                gbuf = [GATE, GATE2]
                gres = [allpt, allmx]
                for hi in range(2):
                    hg = 2 * r + hi

                    def dst_q(tt, hi=hi):
                        return QT[0:64, hi, tt * 512:(tt + 1) * 512], ("qt", hi, tt), 0, 64

                    def extra(tt, b, hi=hi):
                        A("dve", lambda e, b=b, tt=tt, hi=hi: e.tensor_copy(out=gbuf[hi][64:65, tt * 512:(tt + 1) * 512], in_=PS[b][64:65, :]),
                          reads=[("ps", b)], writes=gres[hi])

                    proj_fm(sl, hi * 65, 65, dst_q, extra)
                for hi in range(2):
                    hg = 2 * r + hi
                    G = gbuf[hi]
                    gr = gres[hi]
                    A("act", lambda e, hg=hg, l=l, G=G: e.activation(out=G[64:65, :], in_=G[64:65, :], func=AF.Sigmoid,
                                                                      bias=BFOX[64:65, l * 4 + hg:l * 4 + hg + 1]),
                      reads=gr + [("c", "bfox")], writes=gr)
                    A("act", lambda e, G=G: e.activation(out=G[64:65, :], in_=G[64:65, :], func=AF.Ln),
                      reads=gr, writes=gr)
                    A("dve", lambda e, G=G: e.tensor_tensor_scan(out=G[64:65, :], data0=ONESF[64:65, 0:1].to_broadcast([1, S]),
                                                                  data1=G[64:65, :], initial=0.0, op0=ALU.mult, op1=ALU.add),
                      reads=gr + [("c", "onesf")], writes=gr)
                    A("dve", lambda e, hi=hi, G=G: e.tensor_scalar(out=QT[64:65, hi, :], in0=G[64:65, :], scalar1=8.0, scalar2=None, op0=ALU.mult),
                      reads=gr, writes=[("qaug", hi)])

                def gate_part_b():
                    for hi in range(2):
                        G = gbuf[hi]
                        bn = bank("s")
                        for kb in range(16):
                            mm(PS[bn][:, kb:kb + 1], G[64:65, kb * 128:(kb + 1) * 128], NEGONE[64:65, 0:1], True, True,
                               reads=gres[hi] + [("c", "negone"), ("ps", bn)], writes=[("ps", bn)])
                        A("dve", lambda e, bn=bn, hi=hi: e.tensor_copy(out=NEGCUM[:, hi, :], in_=PS[bn][:, 0:16]),
                          reads=[("ps", bn)], writes=[("negcum", hi)])
                for hi in range(2):
                    def dst_k(tt, hi=hi):
                        return KT[0:64, hi, tt * 512:(tt + 1) * 512], ("kt", hi, tt), 0, 64
                    proj_fm(sl, 130 + hi * 64, 64, dst_k)
                def evac_v(tb0, ntb, b):
                    src = PS[b][:, :].rearrange("p (t c) -> p t c", t=ntb)
                    A("dve", lambda e, src=src, tb0=tb0, ntb=ntb: e.tensor_copy(out=VT[:, tb0:tb0 + ntb, 0:64], in_=src[:, :, 0:64]),
                      reads=[("ps", b)], writes=[("vt", tb0 // 4)])
                    A("dve", lambda e, src=src, tb0=tb0, ntb=ntb: e.tensor_copy(out=VT[:, tb0:tb0 + ntb, 129:193], in_=src[:, :, 64:128]),
                      reads=[("ps", b)], writes=[("vt", tb0 // 4)])
                proj_tm(sl, 258, 128, evac_v)
                gate_part_b()
                sl_next = load_round_weights(l, r + 1)
                steps = []
                for qt in range(NT):
                    obs = [bank("a"), bank("a")]
                    ts = slice(qt * 512, (qt + 1) * 512)
                    for hi in range(2):
                        ob = obs[hi]
                        if hi == 0:
                            pv = [(ob, 65, lambda kb: VT[:, kb, 0:65])]
                            fin = None
                        else:
                            pv = [(ob, 128, lambda kb: VT[:, kb, 65:193])]

                            def fin(obs=obs, ts=ts, qt=qt, sl=sl, r=r):
                                return norm_even_odd(obs[0], obs[1], 512,
                                              lambda f, ts=ts, r=r: f(MIXT[0:64, r, ts], 0, 512),
                                              lambda f, ts=ts, r=r: f(MIXT[64:128, r, ts], 0, 512),
                                              [("mixt", r, qt)])
                        steps += causal_steps(
                            qt,
                            k_ap=lambda kb, hi=hi: KT[0:65, hi, kb * 128:(kb + 1) * 128],
                            q_ap=lambda c0, hi=hi, qt=qt: QT[0:65, hi, qt * 512 + c0:(qt + 1) * 512],
                            krows=65, tab_ap=CMF[:, :], tab_res=("c", "cmf"),
                            bias_fn=lambda kb, hi=hi: (NEGCUM[:, hi, kb:kb + 1], ("negcum", hi)),
                            pv_list=pv, qres=("qt", hi, qt), kres_fn=lambda kb, hi=hi: ("kt", hi, kb // 4),
                            vres_fn=lambda kb, hi=hi: ("vt", kb // 4), xreads=[("qaug", hi), ("kaug",)], fin=fin)
                run_steps(steps, DEFER=6)
                if r == 1:
                    for tt in range(NT):
                        out_proj_tt(sl, 2, tt)

            sl = sl_next
            for h in range(2):
                proj_fm(sl, h * 128, 128, lambda tt, h=h: (QT[:, h, tt * 512:(tt + 1) * 512], ("qt", h, tt), 0, 128))
                proj_fm(sl, 256 + h * 128, 128, lambda tt, h=h: (KT[:, h, tt * 512:(tt + 1) * 512], ("kt", h, tt), 0, 128))

            def evac_vd(tb0, ntb, b):
                src = PS[b][:, :].rearrange("p (t c) -> p t c", t=ntb)
                A("dve", lambda e, src=src, tb0=tb0, ntb=ntb: e.tensor_copy(out=VT[:, tb0:tb0 + ntb, 0:256], in_=src),
                  reads=[("ps", b)], writes=[("vt", tb0 // 4)])
            proj_tm(sl, 512, 256, evac_vd)
            sl_next = load_round_weights(l, 3)
            steps = []
            for qt in range(NT):
                for h in range(2):
                    ts = slice(qt * 512, (qt + 1) * 512)
                    for cc in range(2):
                        ob = bank("a")
                        osum = bank("a")
                        pv = [(ob, 128, lambda kb, h=h: VT[:, kb, h * 128:(h + 1) * 128]),
                              (osum, 128, lambda kb: ONESB[:, :])]

                        def fin(ob=ob, osum=osum, cc=cc, h=h, qt=qt, ts=ts, sl=sl):
                            A("dve", lambda e, osum=osum: e.reciprocal(out=TMP[:, 2, :], in_=PS[osum][:, :]),
                              reads=[("ps", osum)], writes=[("tmp", 2)])
                            A("dve", lambda e, ob=ob, cc=cc: e.tensor_tensor(out=TMP[:, 3 + cc, :], in0=PS[ob][:, :], in1=TMP[:, 2, :], op=ALU.mult),
                              reads=[("ps", ob), ("tmp", 2)], writes=[("tmp", 3 + cc)])
                            if cc == 0:
                                return None
                            A("dve", lambda e: e.scalar_tensor_tensor(out=TMP[:, 5, :], in0=TMP[:, 4, :], scalar=LAMS[:, 5:6], in1=TMP[:, 3, :],
                                                                     op0=ALU.mult, op1=ALU.add),
                              reads=[("tmp", 3), ("tmp", 4), ("neglam",)], writes=[("tmp", 5)])

                            def part2(h=h, ts=ts, qt=qt):
                                A("dve", lambda e: e.tensor_tensor(out=SQB[:, 0, :], in0=TMP[:, 5, :], in1=TMP[:, 5, :], op=ALU.mult),
                                  reads=[("tmp", 5)], writes=[("sqb", 0)])
                                b = bank("m")
                                mm(PS[b][:, :], ONESB[:, :], SQB[:, 0, :], True, True,
                                   reads=[("sqb", 0), ("c", "onesb"), ("ps", b)], writes=[("ps", b)])
                                A("act", lambda e, b=b: e.activation(out=TMP[:, 6, :], in_=PS[b][:, :], func=AF.Sqrt, bias=EPS, scale=1.0 / 128),
                                  reads=[("ps", b)], writes=[("tmp", 6)])

                                def part3(h=h, ts=ts, qt=qt):
                                    A("dve", lambda e: e.reciprocal(out=TMP[:, 7, :], in_=TMP[:, 6, :]),
                                      reads=[("tmp", 6)], writes=[("tmp", 7)])
                                    A("dve", lambda e, h=h, ts=ts: e.scalar_tensor_tensor(out=MIXT[:, h, ts], in0=TMP[:, 5, :], scalar=GDS[:, 0:1], in1=TMP[:, 7, :],
                                                                                         op0=ALU.mult, op1=ALU.mult),
                                      reads=[("tmp", 5), ("tmp", 7), ("gds",)], writes=[("mixt", h, qt)])
                                    return None
                                return part3
                            return part2
                        steps += causal_steps(
                            qt,
                            k_ap=lambda kb, h=h, cc=cc: KT[cc * 64:(cc + 1) * 64, h, kb * 128:(kb + 1) * 128],
                            q_ap=lambda c0, h=h, cc=cc, qt=qt: QT[cc * 64:(cc + 1) * 64, h, qt * 512 + c0:(qt + 1) * 512],
                            krows=64, tab_ap=TDT[:, h, :], tab_res=("c", "td"),
                            bias_fn=lambda kb, h=h, qt=qt: (BDT[:, h * 16 + kb - 4 * qt + 12:h * 16 + kb - 4 * qt + 13], ("c", "bd")),
                            pv_list=pv, qres=("qt", h, qt), kres_fn=lambda kb, h=h: ("kt", h, kb // 4),
                            vres_fn=lambda kb: ("vt", kb // 4), fin=fin)
            run_steps(steps, DEFER=3)
            for tt in range(NT):
                out_proj_tt(sl, 2, tt)

            sl = sl_next
            init_aug(2, False)
            for p in range(2):
                proj_fm(sl, p * 128, 128, lambda tt, p=p: (QT[:, p, tt * 512:(tt + 1) * 512], ("qt", p, tt), 0, 128))
                proj_fm(sl, 256 + p * 128, 128, lambda tt, p=p: (KT[:, p, tt * 512:(tt + 1) * 512], ("kt", p, tt), 0, 128))

            def evac_vc(tb0, ntb, b):
                src = PS[b][:, :].rearrange("p (t c) -> p t c", t=ntb)
                for p in range(2):
                    A("dve", lambda e, src=src, tb0=tb0, ntb=ntb, p=p: e.tensor_copy(
                        out=VT[:, tb0:tb0 + ntb, p * 193:p * 193 + 64], in_=src[:, :, p * 128:p * 128 + 64]),
                      reads=[("ps", b)], writes=[("vt", tb0 // 4)])
                    A("dve", lambda e, src=src, tb0=tb0, ntb=ntb, p=p: e.tensor_copy(
                        out=VT[:, tb0:tb0 + ntb, p * 193 + 129:p * 193 + 193], in_=src[:, :, p * 128 + 64:p * 128 + 128]),
                      reads=[("ps", b)], writes=[("vt", tb0 // 4)])
            proj_tm(sl, 512, 256, evac_vc)
            steps = []
            for qb in range(16):
                js = [j for j in range(5) if qb - 4 + j >= 0]
                pts = {}
                for j in js:
                    kb = qb - 4 + j
                    st = {}

                    def A_(j=j, kb=kb, qb=qb, st=st):
                        b = bank("q")
                        st["b"] = b
                        for h in range(4):
                            p, hh = h // 2, h % 2
                            mm(PS[b][:, h * 128:(h + 1) * 128], KT[hh * 64:(hh + 1) * 64, p, kb * 128:(kb + 1) * 128],
                               QT[hh * 64:(hh + 1) * 64, p, qb * 128:(qb + 1) * 128], True, False,
                               reads=[("kt", p, kb // 4), ("qt", p, qb // 4), ("ps", b)], writes=[("ps", b)])
                            mm(PS[b][:, h * 128:(h + 1) * 128], IDENT[:, :], TC[:, j, h * 128:(h + 1) * 128], False, True,
                               reads=[("c", "ident"), ("tc", j), ("ps", b)], writes=[("ps", b)])

                    def B_(j=j, st=st, pts=pts):
                        b = st["b"]
                        pp = next_pt()
                        pts[j] = pp
                        A("act", lambda e, b=b, pp=pp: e.activation(out=PT[:, pp, :], in_=PS[b][:, :], func=AF.Exp, scale=0.125),
                          reads=[("ps", b)], writes=[("pt", pp)])
                    steps.append([A_, B_, None, None])

                st_c = {}

                def C_(qb=qb, js=js, pts=pts, sl=sl, st_c=st_c):
                    oe = bank("a")
                    oo = bank("a")
                    for p in range(2):
                        for n, j in enumerate(js):
                            kb = qb - 4 + j
                            mm(PS[oe][0:65, p * 128:(p + 1) * 128], VT[:, kb, p * 193:p * 193 + 65], PT[:, pts[j], (2 * p) * 128:(2 * p + 1) * 128],
                               n == 0, n == len(js) - 1,
                               reads=[("pt", pts[j]), ("vt", kb // 4), ("ps", oe)], writes=[("ps", oe)])
                        for n, j in enumerate(js):
                            kb = qb - 4 + j
                            mm(PS[oo][:, p * 128:(p + 1) * 128], VT[:, kb, p * 193 + 65:p * 193 + 193], PT[:, pts[j], (2 * p + 1) * 128:(2 * p + 2) * 128],
                               n == 0, n == len(js) - 1,
                               reads=[("pt", pts[j]), ("vt", kb // 4), ("ps", oo)], writes=[("ps", oo)])

                    def dst_e(f, qb=qb):
                        for p in range(2):
                            f(MIXT[0:64, p, qb * 128:(qb + 1) * 128], p * 128, (p + 1) * 128)

                    def dst_o(f, qb=qb):
                        for p in range(2):
                            f(MIXT[64:128, p, qb * 128:(qb + 1) * 128], p * 128, (p + 1) * 128)
                    st_c["cont"] = norm_even_odd(oe, oo, 256, dst_e, dst_o, [("mixt", 0, qb // 4), ("mixt", 1, qb // 4)])
                steps[-1][2] = C_
                steps[-1][3] = (lambda st_c=st_c: st_c["cont"])
            run_steps(steps)
            out_proj_tt(sl, 2, 0)
            for tt in range(NT):
                if tt + 1 < NT:
                    out_proj_tt(sl, 2, tt + 1)
                rmsnorm(l * 16 + 8, "h", tts=[tt])
            sc.barrier(markers())

            wd = [0]

            def load_wup(fc_):
                wsl_ = fc_ % 3
                dma("pool", WUP[:, wsl_, :, :], wup[l, :, fc_ % NFC, :, :], "wu%d" % wsl_, writes=[("wup", wsl_)])

            def load_wdn(g_, dc_):
                dsl_ = wd[0] % 4
                wd[0] += 1
                dma("pool", WDN[:, dsl_, :, :], wdn[l, :, g_, dc_, :, :], "wd%d" % dsl_, writes=[("wdn", dsl_)])
                return dsl_

            def ybuf(par, sub, t2):
                return YB[:, sub, t2, :] if par == 0 else TMP[:, sub * 2 + t2, :]

            def silbuf(par, t2):
                return SIL[:, t2, :] if par == 0 else TMP[:, 4 + t2, :]

            nup = 2 * NFC
            for sub_ in range(2):
                for usl_ in range(2):
                    A("dve", lambda e, sub_=sub_, usl_=usl_: e.memset(UVR[:, sub_, usl_, 0:2], 0.0),
                      writes=[("uvr", sub_, usl_, "h")])
            load_wup(0)
            load_wup(1)
            dslots_all = {}

            def up_fc(it):
                hh = it // NFC
                fc = it % NFC
                g = fc // 11
                fi = fc % 11
                G = it // 11
                asl = it % 12
                wsl = it % 3
                if it + 2 < nup:
                    load_wup(it + 2)
                if fi >= 7:
                    dslots_all[(G, fi - 7)] = load_wdn(g, fi - 7)
                usl = fc % 2
                par = fc % 2
                for sub in range(2):
                    cp = sub * 22 + fc
                    if hh == 1:
                        if sub == 0:
                            A("dve", lambda e, sub=sub, usl=usl, cp=cp: e.tensor_copy(out=UVR[:, sub, usl, 0:2], in_=HALO[:, cp, :]),
                              reads=[("halo", cp)], writes=[("uvr", sub, usl, "h")])
                        else:
                            A("act", lambda e, sub=sub, usl=usl, cp=cp: e.activation(out=UVR[:, sub, usl, 0:2], in_=HALO[:, cp, :], func=AF.Copy),
                              reads=[("halo", cp)], writes=[("uvr", sub, usl, "h")])
                    bks = []
                    for t2 in range(2):
                        tt = hh * 2 + t2
                        ts = slice(tt * 512, (tt + 1) * 512)
                        b = bank("u")
                        bks.append(b)
                        for c in range(KC):
                            mm(PS[b][:, :], WUP[:, wsl, c, sub * 128:(sub + 1) * 128], hT[:, c, ts], c == 0, c == KC - 1,
                               reads=[("wup", wsl), ("hT", c, tt), ("ps", b)], writes=[("ps", b)])
                    for t2 in range(2):
                        b = bks[t2]
                        lo = 2 + t2 * 512
                        A("act", lambda e, b=b, sub=sub, usl=usl, lo=lo: e.activation(out=UVR[:, sub, usl, lo:lo + 512], in_=PS[b][:, :], func=AF.Copy),
                          reads=[("ps", b)], writes=[("uvr", sub, usl, t2)])
                    for t2 in range(2):
                        lo = 2 + t2 * 512
                        ysl = t2
                        A("act", lambda e, sub=sub, usl=usl, lo=lo, ysl=ysl, cp=cp, l=l, par=par: e.activation(
                            out=ybuf(par, sub, ysl), in_=UVR[:, sub, usl, lo - 2:lo + 510], func=AF.Identity,
                            scale=CONVP[:, l, cp, 0:1], bias=CONVP[:, l, cp, 3:4]),
                          reads=[("uvr", sub, usl, 0), ("uvr", sub, usl, 1), ("uvr", sub, usl, "h"), ("c", "convp")], writes=[("yb", par, sub, ysl)])
                    for t2 in range(2):
                        lo = 2 + t2 * 512
                        ysl = t2
                        A("dve", lambda e, sub=sub, usl=usl, lo=lo, ysl=ysl, cp=cp, l=l, par=par: e.scalar_tensor_tensor(
                            out=ybuf(par, sub, ysl), in0=UVR[:, sub, usl, lo - 1:lo + 511], scalar=CONVP[:, l, cp, 1:2], in1=ybuf(par, sub, ysl),
                            op0=ALU.mult, op1=ALU.add),
                          reads=[("uvr", sub, usl, 0), ("uvr", sub, usl, 1), ("uvr", sub, usl, "h"), ("yb", par, sub, ysl), ("c", "convp")], writes=[("yb", par, sub, ysl)])
                    for t2 in range(2):
                        b = bks[t2]
                        ysl = t2
                        A("dve", lambda e, b=b, sub=sub, ysl=ysl, cp=cp, l=l, par=par: e.scalar_tensor_tensor(
                            out=ybuf(par, sub, ysl), in0=PS[b][:, :], scalar=CONVP[:, l, cp, 2:3], in1=ybuf(par, sub, ysl),
                            op0=ALU.mult, op1=ALU.add),
                          reads=[("ps", b), ("yb", par, sub, ysl), ("c", "convp")], writes=[("yb", par, sub, ysl)])
                    if hh == 0:
                        A("act", lambda e, sub=sub, usl=usl, cp=cp: e.activation(out=HALO[:, cp, :], in_=UVR[:, sub, usl, 1024:1026], func=AF.Copy),
                          reads=[("uvr", sub, usl, 1)], writes=[("halo", cp)])
                for t2 in range(2):
                    A("act", lambda e, t2=t2, par=par: e.activation(out=silbuf(par, t2), in_=ybuf(par, 1, t2), func=AF.Silu),
                      reads=[("yb", par, 1, t2)], writes=[("sil", par, t2)])
                    A("pool" if POOL_FFN else "dve", lambda e, t2=t2, asl=asl, par=par: e.tensor_tensor(out=ACTT[:, asl, t2 * 512:(t2 + 1) * 512], in0=silbuf(par, t2), in1=ybuf(par, 0, t2), op=ALU.mult),
                      reads=[("sil", par, t2), ("yb", par, 0, t2)], writes=[("actt", asl, t2)])

            def down(G):
                hh = G // 2
                g = G % 2
                for dc in range(KC):
                    dsl = dslots_all[(G, dc)]
                    for t2 in range(2):
                        tt = hh * 2 + t2
                        ts = slice(tt * 512, (tt + 1) * 512)
                        b = bank("d")
                        for fi in range(11):
                            asl = (G * 11 + fi) % 12
                            mm(PS[b][:, :], WDN[:, dsl, fi, :], ACTT[:, asl, t2 * 512:(t2 + 1) * 512], fi == 0, fi == 10,
                               reads=[("wdn", dsl), ("actt", asl, t2), ("ps", b)], writes=[("ps", b)])
                        A("dve", lambda e, b=b, dc=dc, ts=ts: e.tensor_tensor(out=xT[:, dc, ts], in0=PS[b][:, :], in1=xT[:, dc, ts], op=ALU.add),
                          reads=[("ps", b), ("xT", dc, tt)], writes=[("xT", dc, tt)])
                    if dc + 4 < KC:
                        dslots_all[(G, dc + 4)] = load_wdn(g, dc + 4)

            done_up = set()
            for G in range(4):
                for fi in range(11):
                    it = G * 11 + fi
                    if it not in done_up:
                        up_fc(it)
                        done_up.add(it)
                if G < 3:
                    up_fc((G + 1) * 11)
                    done_up.add((G + 1) * 11)
                down(G)
            sc.barrier(markers())
        for tt in range(NT):
            if final_norm:
                rmsnorm(32, "x", tts=[tt])
            dma("sp", out_t[s, :, :, tt * 512:(tt + 1) * 512], xT[:, :, tt * 512:(tt + 1) * 512], "o%d" % tt,
                reads=[("xT", c, tt) for c in range(KC)])

    sc.emit(final_lanes=["o%d" % tt for tt in range(NT)])
    return nc


def _prep_inputs(inputs):
    f = lambda a: np.ascontiguousarray(a, dtype=np.float32)
    perm = _win_perm()
    cst = _host_consts()
    w_in = np.asarray(inputs["w_in"])
    win = f(w_in[:, :, perm].reshape(2, KC, 128, 2308).transpose(0, 2, 1, 3))
    wout = f(np.asarray(inputs["w_out"]).reshape(2, 6, 128, D).transpose(0, 2, 1, 3))
    wu = np.asarray(inputs["w_ffn_in"]).reshape(2, KC, 128, 2, NFC, 128)
    wup = f(wu.transpose(0, 2, 4, 1, 3, 5).reshape(2, 128, NFC, KC, 256))
    wd = np.asarray(inputs["w_ffn_out"]).reshape(2, 2, 11, 128, 8, 128)
    wdn = f(wd.transpose(0, 3, 1, 4, 2, 5))
    gs = [np.asarray(inputs["g_mix"])[0], np.asarray(inputs["g_ffn"])[0], np.asarray(inputs["g_mix"])[1],
          np.asarray(inputs["g_ffn"])[1], np.asarray(inputs["g_final"])]
    gcols = f(np.concatenate([g.reshape(KC, 128).T for g in gs], axis=1))
    cw = np.asarray(inputs["conv_w"]).reshape(2, 3, 44, 128)
    cb = np.asarray(inputs["conv_b"]).reshape(2, 1, 44, 128)
    convp = f(np.concatenate([cw, cb], axis=1).transpose(3, 0, 2, 1))
    gdiff = f(np.asarray(inputs["g_diff"]).T)
    bfox = f(np.asarray(inputs["b_fox_f"]).reshape(1, 8))
    dlam = f(np.broadcast_to(np.asarray(inputs["diff_lambda"]).reshape(1, 2, 256), (128, 2, 256)))
    rel = np.asarray(inputs["rel_bias"])
    relg = f(rel[:, :, cst["relidx"]].transpose(0, 3, 2, 1, 4))
    shared = dict(win=win, wout=wout, wup=wup, wdn=wdn, gcols=gcols, convp=convp, gdiff=gdiff, bfox=bfox,
                  dlam=dlam, relg=relg, relmask=f(cst["relmask"]), ident=f(cst["ident"]), cmf=f(cst["cmf"]),
                  td=f(cst["td"]), bd=f(cst["bd"]))
    return shared


def _x_shard(x, b0, n):
    xs = np.asarray(x[b0:b0 + n], dtype=np.float32)
    return np.ascontiguousarray(xs.transpose(0, 2, 1).reshape(n, KC, 128, S).transpose(0, 2, 1, 3))


_PROG = {}


def kernel(**inputs):
    x = np.asarray(inputs["x"])
    B = x.shape[0]
    ncores = 8
    per = B // ncores
    shared = _prep_inputs(inputs)
    key = (per, (0, 1), True)
    if key not in _PROG:
        _PROG[key] = build_program(per, (0, 1), True)
    nc = _PROG[key]
    in_maps = []
    for c in range(ncores):
        m = dict(shared)
        m["xT_in"] = _x_shard(x, c * per, per)
        in_maps.append(m)
    res = run_bass_kernel_spmd(nc, in_maps, core_ids=list(range(ncores)))
    out = np.empty((B, S, D), np.float32)
    for c in range(ncores):
        o = res.results[c]["outT"]
        out[c * per:(c + 1) * per] = o.transpose(0, 2, 1, 3).reshape(per, D, S).transpose(0, 2, 1)
    return out
```
